# Optimizing a Trainium2 kernel written in Bass

```python
import math
import jax
import jax.numpy as jnp
from jax import lax
import numpy as np

D_MODEL = 1024
BATCH = 32
SEQ = 2048
DEPTH = 4
DEC_BATCH = 16
DEC_SEQ = 16
PAST_LEN = 2048

CHUNK = 64
N_META = 16
Q_BLOCK = 128
EPS = 1e-6
NEG_INF = -1e30

DA_HEADS = 4
DA_HEAD_DIM = 64
DA_VDIM = 2 * DA_HEAD_DIM
DA_QK_WIDTH = DA_HEADS * 2 * DA_HEAD_DIM
DA_WIDTH = DA_HEADS * DA_VDIM

SSM_D_INNER = D_MODEL // 2
SSM_HEAD_DIM = 64
SSM_HEADS = SSM_D_INNER // SSM_HEAD_DIM
SSM_GROUPS = 2
SSM_D_STATE = 128
SSM_CONV = 4
SSM_CHUNK = 64
SSM_CONV_DIM = SSM_D_INNER + 2 * SSM_GROUPS * SSM_D_STATE

SB_HEADS = 8
SB_HEAD_DIM = 64
SB_WIDTH = SB_HEADS * SB_HEAD_DIM

D_MIX = DA_WIDTH + SSM_D_INNER + SB_WIDTH

D_FF = 2816
FFN_CONV = 3

REL_BUCKETS = 32
REL_MAX_DIST = 128

OFF_DA_K = DA_QK_WIDTH
OFF_DA_V = 2 * DA_QK_WIDTH
OFF_SSM_Z = 2 * DA_QK_WIDTH + DA_WIDTH
OFF_SSM_XBC = OFF_SSM_Z + SSM_D_INNER
OFF_SSM_DT = OFF_SSM_XBC + SSM_CONV_DIM
OFF_SB_Q = OFF_SSM_DT + SSM_HEADS
OFF_SB_K = OFF_SB_Q + SB_WIDTH
OFF_SB_V = OFF_SB_K + SB_WIDTH
D_IN_PROJ = OFF_SB_V + SB_WIDTH

kernel_name = 'hybrid_diffattn_ssd_stickbreaking_stream_step'


def rmsnorm(x, g):
    xf = x.astype(jnp.float32)
    y = xf * lax.rsqrt(jnp.mean(xf * xf, axis=-1, keepdims=True) + EPS)
    return (y * g.astype(jnp.float32)).astype(x.dtype)


def causal_dwconv(x_ext, w, b):
    y = lax.conv_general_dilated(x_ext, w[:, None, :].astype(x_ext.dtype), window_strides=(1,),
                                 padding='VALID', dimension_numbers=('NWC', 'WIO', 'NWC'),
                                 feature_group_count=x_ext.shape[-1])
    return y + b.astype(y.dtype)


def rel_bucket(rel):
    half = REL_BUCKETS // 2
    max_exact = half // 2
    n = jnp.abs(rel)
    nf = jnp.maximum(n, 1).astype(jnp.float32)
    large = max_exact + (jnp.log(nf / max_exact) / math.log(REL_MAX_DIST / max_exact)
                         * (half - max_exact)).astype(jnp.int32)
    large = jnp.minimum(large, half - 1)
    return jnp.where(rel > 0, half, 0) + jnp.where(n < max_exact, n, large)


def prompt_chunk_id(pos):
    return jnp.where(pos < N_META, -1, (pos - N_META) // CHUNK)


def to_blocks(a, n_blk):
    pad = n_blk * Q_BLOCK - a.shape[1]
    a = jnp.pad(a, [(0, 0), (0, pad)] + [(0, 0)] * (a.ndim - 2))
    return jnp.moveaxis(a.reshape((a.shape[0], n_blk, Q_BLOCK) + a.shape[2:]), 1, 0)


def from_blocks(a, L):
    a = jnp.moveaxis(a, 0, 1)
    return a.reshape((a.shape[0], -1) + a.shape[3:])[:, :L]


def diff_attn_block(q, k, v, q_pos, q_chunk, k_pos, k_chunk, rel_table, lam):
    scale = DA_HEAD_DIM ** -0.5
    bias = jnp.transpose(rel_table[rel_bucket(k_pos[None, :] - q_pos[:, None])], (2, 0, 1))
    mask = k_chunk[None, :] <= q_chunk[:, None]
    s = jnp.einsum('bqhmd,bkhmd->bmhqk', q, k).astype(jnp.float32) * scale
    s = jnp.where(mask, s + bias[None, None].astype(jnp.float32), NEG_INF)
    p = jax.nn.softmax(s, axis=-1)
    p = p[:, 0] - lam * p[:, 1]
    return jnp.einsum('bhqk,bkhv->bqhv', p.astype(v.dtype), v)


def stick_breaking_block(q, k, v, q_pos, k_pos):
    z = jnp.einsum('bqhd,bkhd->bhqk', q, k).astype(jnp.float32) * (SB_HEAD_DIM ** -0.5)
    mask = k_pos[None, :] < q_pos[:, None]
    log_keep = jnp.where(mask, jax.nn.log_sigmoid(-z), 0.0)
    tail = lax.cumsum(log_keep, axis=3, reverse=True) - log_keep
    w = jnp.where(mask, jnp.exp(jax.nn.log_sigmoid(z) + tail), 0.0)
    return jnp.einsum('bhqk,bkhd->bqhd', w.astype(v.dtype), v)


def ssd_scan(x, dt, a, bmat, cmat, h0):
    b, L = x.shape[:2]
    nc = -(-L // SSM_CHUNK)
    pad = nc * SSM_CHUNK - L
    R = SSM_HEADS // SSM_GROUPS

    def pad_l(t):
        return jnp.pad(t, [(0, 0), (0, pad)] + [(0, 0)] * (t.ndim - 2))

    xdt = pad_l(x.astype(jnp.float32) * dt[..., None]).reshape(b, nc, SSM_CHUNK, SSM_GROUPS, R, SSM_HEAD_DIM)
    ld = pad_l(dt * a).reshape(b, nc, SSM_CHUNK, SSM_GROUPS, R)
    bm = pad_l(bmat.astype(jnp.float32)).reshape(b, nc, SSM_CHUNK, SSM_GROUPS, SSM_D_STATE)
    cm = pad_l(cmat.astype(jnp.float32)).reshape(b, nc, SSM_CHUNK, SSM_GROUPS, SSM_D_STATE)
    cs = jnp.cumsum(jnp.transpose(ld, (0, 3, 4, 1, 2)), axis=-1)
    tril = jnp.tril(jnp.ones((SSM_CHUNK, SSM_CHUNK), dtype=bool))
    lmat = jnp.exp(jnp.where(tril, cs[..., :, None] - cs[..., None, :], -jnp.inf))
    cb = jnp.einsum('bclgn,bcsgn->bgcls', cm, bm)
    y_diag = jnp.einsum('bgcls,bgrcls,bcsgrp->bclgrp', cb, lmat, xdt)
    decay_to_end = jnp.exp(cs[..., -1:] - cs)
    chunk_states = jnp.einsum('bcsgn,bgrcs,bcsgrp->cbgrpn', bm, decay_to_end, xdt)
    chunk_decay = jnp.moveaxis(jnp.exp(cs[..., -1]), 3, 0)

    def step(h, inp):
        dec, st = inp
        return dec[..., None, None] * h + st, h

    h0g = h0.astype(jnp.float32).reshape(b, SSM_GROUPS, R, SSM_HEAD_DIM, SSM_D_STATE)
    h_final, h_prev = lax.scan(step, h0g, (chunk_decay, chunk_states))
    y_off = jnp.einsum('bclgn,cbgrpn,bgrcl->bclgrp', cm, h_prev, jnp.exp(cs))
    y = (y_diag + y_off).reshape(b, nc * SSM_CHUNK, SSM_HEADS, SSM_HEAD_DIM)[:, :L]
    h_final = h_final.reshape(b, SSM_HEADS, SSM_HEAD_DIM, SSM_D_STATE)
    return y.astype(x.dtype), h_final.astype(h0.dtype)


def layer(x, past, p, lam_init, rel_table):
    (w_in, w_out, norm_mix, norm_ffn, lq1, lk1, lq2, lk2, da_subln,
     conv_w, conv_b, dt_bias, a_log, d_skip, ssm_norm,
     w_gate, w_up, w_down, fconv_w, fconv_b) = p
    b, L, _ = x.shape
    h = rmsnorm(x, norm_mix)
    proj = h @ w_in
    da_q, da_k, da_v, ssm_z, ssm_xbc, ssm_dt, sb_q, sb_k, sb_v = jnp.split(
        proj, [OFF_DA_K, OFF_DA_V, OFF_SSM_Z, OFF_SSM_XBC, OFF_SSM_DT, OFF_SB_Q, OFF_SB_K, OFF_SB_V], axis=-1)
    da_q = da_q.reshape(b, L, DA_HEADS, 2, DA_HEAD_DIM)
    da_k = da_k.reshape(b, L, DA_HEADS, 2, DA_HEAD_DIM)
    da_v = da_v.reshape(b, L, DA_HEADS, DA_VDIM)
    sb_q = sb_q.reshape(b, L, SB_HEADS, SB_HEAD_DIM)
    sb_k = sb_k.reshape(b, L, SB_HEADS, SB_HEAD_DIM)
    sb_v = sb_v.reshape(b, L, SB_HEADS, SB_HEAD_DIM)
    lam = (jnp.exp(jnp.sum(lq1.astype(jnp.float32) * lk1.astype(jnp.float32)))
           - jnp.exp(jnp.sum(lq2.astype(jnp.float32) * lk2.astype(jnp.float32))) + lam_init)

    if past is None:
        pos = jnp.arange(L, dtype=jnp.int32)
        chunk = prompt_chunk_id(pos)
        n_blk = -(-L // Q_BLOCK)
        qpos_b = jnp.arange(n_blk * Q_BLOCK, dtype=jnp.int32).reshape(n_blk, Q_BLOCK)
        da_o = from_blocks(lax.map(
            lambda blk: diff_attn_block(blk[0], da_k, da_v, blk[1], prompt_chunk_id(blk[1]), pos, chunk, rel_table, lam),
            (to_blocks(da_q, n_blk), qpos_b)), L)
        sb_o = from_blocks(lax.map(
            lambda blk: stick_breaking_block(blk[0], sb_k, sb_v, blk[1], pos),
            (to_blocks(sb_q, n_blk), qpos_b)), L)
        conv_in = jnp.pad(ssm_xbc, ((0, 0), (SSM_CONV - 1, 0), (0, 0)))
        h0 = jnp.zeros((b, SSM_HEADS, SSM_HEAD_DIM, SSM_D_STATE), x.dtype)
    else:
        c_da_k, c_da_v, c_sb_k, c_sb_v, h0, conv_hist, _ = past
        past_len = c_da_k.shape[1]
        kpos = jnp.arange(past_len + L, dtype=jnp.int32)
        qpos = kpos[past_len:]
        da_o = diff_attn_block(da_q, jnp.concatenate([c_da_k, da_k], axis=1), jnp.concatenate([c_da_v, da_v], axis=1),
                               qpos, qpos // CHUNK, kpos, kpos // CHUNK, rel_table, lam)
        sb_o = stick_breaking_block(sb_q, jnp.concatenate([c_sb_k, sb_k], axis=1),
                                    jnp.concatenate([c_sb_v, sb_v], axis=1), qpos, kpos)
        conv_in = jnp.concatenate([conv_hist, ssm_xbc], axis=1)

    da_o = (rmsnorm(da_o, da_subln) * (1.0 - lam_init)).reshape(b, L, DA_WIDTH)

    new_conv = conv_in[:, -(SSM_CONV - 1):]
    xbc = jax.nn.silu(causal_dwconv(conv_in, conv_w, conv_b))
    ssm_x, ssm_b, ssm_c = jnp.split(xbc, [SSM_D_INNER, SSM_D_INNER + SSM_GROUPS * SSM_D_STATE], axis=-1)
    ssm_x = ssm_x.reshape(b, L, SSM_HEADS, SSM_HEAD_DIM)
    dt = jax.nn.softplus(ssm_dt.astype(jnp.float32) + dt_bias.astype(jnp.float32))
    a = -jnp.exp(a_log.astype(jnp.float32))
    y, h_final = ssd_scan(ssm_x, dt, a, ssm_b.reshape(b, L, SSM_GROUPS, SSM_D_STATE),
                          ssm_c.reshape(b, L, SSM_GROUPS, SSM_D_STATE), h0)
    y = (y + d_skip[:, None] * ssm_x).reshape(b, L, SSM_D_INNER) * jax.nn.silu(ssm_z)
    y = rmsnorm(y.reshape(b, L, SSM_GROUPS, SSM_D_INNER // SSM_GROUPS),
                ssm_norm.reshape(SSM_GROUPS, SSM_D_INNER // SSM_GROUPS)).reshape(b, L, SSM_D_INNER)

    mix = jnp.concatenate([da_o, y, sb_o.reshape(b, L, SB_WIDTH)], axis=-1)
    x = x + mix @ w_out

    h2 = rmsnorm(x, norm_ffn)
    g = h2 @ w_gate
    u = h2 @ w_up
    if past is None:
        g_ext = jnp.pad(g, ((0, 0), (FFN_CONV - 1, 0), (0, 0)))
    else:
        g_ext = jnp.concatenate([past[6], g], axis=1)
    new_ffn = g_ext[:, -(FFN_CONV - 1):]
    g = causal_dwconv(g_ext, fconv_w, fconv_b)
    x = x + (jax.nn.silu(g) * u) @ w_down
    return x, (da_k, da_v, sb_k, sb_v, h_final, new_conv, new_ffn)


def stack_states(states):
    return tuple(jnp.stack([s[j] for s in states], axis=0) for j in range(7))


def setup_inputs(seed: int = 0) -> dict:
    key = jax.random.key(seed)
    ks = jax.random.split(key, 40)

    def nrm(i, shape, scale=1.0):
        return jax.random.normal(ks[i], shape, jnp.float32) * scale

    dt0 = jnp.exp(jax.random.uniform(ks[20], (DEPTH, SSM_HEADS), jnp.float32,
                                     math.log(0.001), math.log(0.1)))
    dt_bias = dt0 + jnp.log(-jnp.expm1(-dt0))
    a_log = jnp.log(jax.random.uniform(ks[21], (DEPTH, SSM_HEADS), jnp.float32, 1.0, 16.0))
    return {
        'x_prompt': nrm(0, (BATCH, SEQ, D_MODEL)),
        'x_sample': nrm(1, (DEC_BATCH, DEC_SEQ, D_MODEL)),
        'cache_da_k': nrm(2, (DEPTH, DEC_BATCH, PAST_LEN, DA_HEADS, 2, DA_HEAD_DIM)),
        'cache_da_v': nrm(3, (DEPTH, DEC_BATCH, PAST_LEN, DA_HEADS, DA_VDIM)),
        'cache_sb_k': nrm(4, (DEPTH, DEC_BATCH, PAST_LEN, SB_HEADS, SB_HEAD_DIM)),
        'cache_sb_v': nrm(5, (DEPTH, DEC_BATCH, PAST_LEN, SB_HEADS, SB_HEAD_DIM)),
        'state_ssm': nrm(6, (DEPTH, DEC_BATCH, SSM_HEADS, SSM_HEAD_DIM, SSM_D_STATE), 0.5),
        'state_ssm_conv': nrm(7, (DEPTH, DEC_BATCH, SSM_CONV - 1, SSM_CONV_DIM)),
        'state_ffn_conv': nrm(8, (DEPTH, DEC_BATCH, FFN_CONV - 1, D_FF)),
        'meta_tokens': nrm(9, (N_META, D_MODEL)),
        'rel_bias_table': nrm(10, (REL_BUCKETS, DA_HEADS), 0.5),
        'w_in': nrm(11, (DEPTH, D_MODEL, D_IN_PROJ), D_MODEL ** -0.5),
        'w_out': nrm(12, (DEPTH, D_MIX, D_MODEL), D_MIX ** -0.5),
        'norm_mix': 1.0 + nrm(13, (DEPTH, D_MODEL), 0.05),
        'norm_ffn': 1.0 + nrm(14, (DEPTH, D_MODEL), 0.05),
        'da_lambda_q1': nrm(15, (DEPTH, DA_HEAD_DIM), 0.1),
        'da_lambda_k1': nrm(16, (DEPTH, DA_HEAD_DIM), 0.1),
        'da_lambda_q2': nrm(17, (DEPTH, DA_HEAD_DIM), 0.1),
        'da_lambda_k2': nrm(18, (DEPTH, DA_HEAD_DIM), 0.1),
        'da_subln': 1.0 + nrm(19, (DEPTH, DA_VDIM), 0.05),
        'ssm_conv_w': nrm(22, (DEPTH, SSM_CONV, SSM_CONV_DIM), SSM_CONV ** -0.5),
        'ssm_conv_b': nrm(23, (DEPTH, SSM_CONV_DIM), 0.01),
        'ssm_dt_bias': dt_bias,
        'ssm_a_log': a_log,
        'ssm_d': 1.0 + nrm(24, (DEPTH, SSM_HEADS), 0.05),
        'ssm_norm': 1.0 + nrm(25, (DEPTH, SSM_D_INNER), 0.05),
        'ffn_w_gate': nrm(26, (DEPTH, D_MODEL, D_FF), D_MODEL ** -0.5),
        'ffn_w_up': nrm(27, (DEPTH, D_MODEL, D_FF), D_MODEL ** -0.5),
        'ffn_w_down': nrm(28, (DEPTH, D_FF, D_MODEL), D_FF ** -0.5),
        'ffn_conv_w': nrm(29, (DEPTH, FFN_CONV, D_FF), FFN_CONV ** -0.5),
        'ffn_conv_b': nrm(30, (DEPTH, D_FF), 0.01),
        'final_norm': 1.0 + nrm(31, (D_MODEL,), 0.05),
    }


def reference(x_prompt, x_sample, cache_da_k, cache_da_v, cache_sb_k, cache_sb_v, state_ssm,
              state_ssm_conv, state_ffn_conv, meta_tokens, rel_bias_table, w_in, w_out, norm_mix,
              norm_ffn, da_lambda_q1, da_lambda_k1, da_lambda_q2, da_lambda_k2, da_subln, ssm_conv_w,
              ssm_conv_b, ssm_dt_bias, ssm_a_log, ssm_d, ssm_norm, ffn_w_gate, ffn_w_up, ffn_w_down,
              ffn_conv_w, ffn_conv_b, final_norm):
    b = x_prompt.shape[0]
    meta = jnp.broadcast_to(meta_tokens[None].astype(x_prompt.dtype), (b, N_META, D_MODEL))
    xp = jnp.concatenate([meta, x_prompt], axis=1)
    xs = x_sample
    p_states = []
    s_states = []
    for i in range(DEPTH):
        params = (w_in[i], w_out[i], norm_mix[i], norm_ffn[i], da_lambda_q1[i], da_lambda_k1[i],
                  da_lambda_q2[i], da_lambda_k2[i], da_subln[i], ssm_conv_w[i], ssm_conv_b[i],
                  ssm_dt_bias[i], ssm_a_log[i], ssm_d[i], ssm_norm[i], ffn_w_gate[i], ffn_w_up[i],
                  ffn_w_down[i], ffn_conv_w[i], ffn_conv_b[i])
        lam_init = 0.8 - 0.6 * math.exp(-0.3 * i)
        xp, st_p = layer(xp, None, params, lam_init, rel_bias_table)
        past = (cache_da_k[i], cache_da_v[i], cache_sb_k[i], cache_sb_v[i], state_ssm[i],
                state_ssm_conv[i], state_ffn_conv[i])
        xs, st_s = layer(xs, past, params, lam_init, rel_bias_table)
        p_states.append(st_p)
        s_states.append(st_s)
    y_prompt = rmsnorm(xp[:, N_META:], final_norm)
    y_sample = rmsnorm(xs, final_norm)
    p_da_k, p_da_v, p_sb_k, p_sb_v, p_ssm, p_ssm_conv, p_ffn_conv = stack_states(p_states)
    s_da_k, s_da_v, s_sb_k, s_sb_v, s_ssm, s_ssm_conv, s_ffn_conv = stack_states(s_states)
    return (y_prompt, y_sample, p_da_k, p_da_v, p_sb_k, p_sb_v, p_ssm, p_ssm_conv, p_ffn_conv,
            s_da_k, s_da_v, s_sb_k, s_sb_v, s_ssm, s_ssm_conv, s_ffn_conv)
```

```python
import math
from contextlib import ExitStack
import numpy as np
import concourse.bass as bass
import concourse.mybir as mybir
from concourse.bass_utils import run_bass_kernel_spmd

F32 = mybir.dt.float32
BF16 = mybir.dt.bfloat16
AF = mybir.ActivationFunctionType
ALU = mybir.AluOpType
AX = mybir.AxisListType

D = 1024; DEPTH = 4; NB = 32; SEQ = 2048; NSB = 16; SSEQ = 16; PAST = 2048
NMETA = 16; LP = SEQ + NMETA
DIN = 4616; DMIX = 1536; DFF = 2816
OFF_DA_K = 512; OFF_DA_V = 1024; OFF_Z = 1536; OFF_XBC = 2048; OFF_DT = 3072; OFF_SB_Q = 3080; OFF_SB_K = 3592; OFF_SB_V = 4104
NT = 2080
EPS = 1e-6
NEG = -30000.0
NCORES = 8
SEM_LIMIT = 30000


class Prog:
    def __init__(self):
        self.ins = []
        self.last_w = {}
        self.readers = {}
        self.last_stream = {}

    def op(self, eng, fn, reads=(), writes=(), stream=None):
        i = len(self.ins)
        psr = [k for k in reads if k.startswith("ps")]
        if psr:
            reads = [k for k in reads if not k.startswith("ps")]
            writes = list(writes) + psr
        dom = ('dma', stream) if stream is not None else ('eng', eng)
        deps = set()
        for k in reads:
            w = self.last_w.get(k)
            if w is not None:
                deps.add(w)
        for k in writes:
            w = self.last_w.get(k)
            if w is not None:
                deps.add(w)
            deps.update(self.readers.get(k, {}).values())
        if stream is not None:
            p = self.last_stream.get(stream)
            if p is not None:
                deps.add(p)
            self.last_stream[stream] = i
        best = {}
        for d in deps:
            dd = self.ins[d]
            if dd['dom'] == dom and dom == ('eng', 'pe'):
                continue
            if dd['dom'] not in best or best[dd['dom']] < d:
                best[dd['dom']] = d
        for d in best.values():
            self.ins[d]['needs_inc'] = True
        self.ins.append(dict(eng=eng, fn=fn, deps=sorted(best.values()), dom=dom, needs_inc=False))
        for k in reads:
            self.readers.setdefault(k, {})[dom] = i
        for k in writes:
            self.last_w[k] = i
            self.readers[k] = {}
        return i

    def finish(self, eng='sp'):
        deps = list(self.last_stream.values())
        for d in deps:
            self.ins[d]['needs_inc'] = True
        self.ins.append(dict(eng=eng, fn=None, deps=sorted(deps), dom=('eng', eng), needs_inc=False))

    def emit(self, nc, es):
        sems = {}
        cnt = {}
        nsem = [0]

        def get_sem(dom, ep):
            key = (dom, ep)
            if key not in sems:
                sems[key] = es.enter_context(nc.semaphore("s%d" % nsem[0]))
                nsem[0] += 1
            return key

        known = {}
        plan = {e: [] for e in ('pe', 'act', 'dve', 'pool', 'sp')}
        for it in self.ins:
            waits = []
            kn = known.setdefault(it['eng'], {})
            for d in it['deps']:
                sk, val = self.ins[d]['sem']
                if kn.get(sk, 0) >= val:
                    continue
                kn[sk] = val
                waits.append((sk, val))
            inc = None
            if it['needs_inc']:
                dom = it['dom']
                step = 16 if dom[0] == 'dma' else 1
                ep, v = cnt.get(dom, (0, 0))
                if v + step > SEM_LIMIT:
                    ep, v = ep + 1, 0
                v += step
                cnt[dom] = (ep, v)
                sk = get_sem(dom, ep)
                it['sem'] = (sk, v)
                inc = (sk, step)
            plan[it['eng']].append((it['fn'], waits, inc))
        self.nsem = nsem[0]
        block = es.enter_context(nc.Block())

        def run(e, lst):
            for fn, waits, inc in lst:
                for sk, val in waits:
                    e.wait_ge(sems[sk], val)
                if fn is not None:
                    r = fn(e)
                    if inc is not None:
                        r.then_inc(sems[inc[0]], inc[1])

        @block.tensor
        def _(e):
            run(e, plan['pe'])

        @block.scalar
        def _(e):
            run(e, plan['act'])

        @block.vector
        def _(e):
            run(e, plan['dve'])

        @block.gpsimd
        def _(e):
            run(e, plan['pool'])

        @block.sync
        def _(e):
            run(e, plan['sp'])


def rel_bucket_np(rel):
    half = 16; max_exact = 8
    n = np.abs(rel)
    nf = np.maximum(n, 1).astype(np.float32)
    large = max_exact + (np.log(nf / max_exact) / math.log(128 / max_exact) * (half - max_exact)).astype(np.int32)
    large = np.minimum(large, half - 1)
    return np.where(rel > 0, half, 0) + np.where(n < max_exact, n, large)


def host_consts():
    c = {}
    r = np.arange(-255, 257)
    b = rel_bucket_np(r)
    eh = np.zeros((32, 512), np.float32)
    eh[b, np.arange(512)] = 1.0
    c['c_eh'] = eh
    p = np.arange(128)
    c['c_ident'] = np.eye(128, dtype=np.float32)
    c['c_antiid'] = np.eye(128, dtype=np.float32)[:, ::-1].copy()
    kl = p[:, None]; ql = p[None, :]
    c['c_damask'] = np.where((kl // 64) > (ql // 64), NEG, 0.0).astype(np.float32)
    c['c_sbmask01'] = (kl < ql).astype(np.float32)
    c['c_sbmaskneg'] = np.where(kl < ql, 0.0, NEG).astype(np.float32)
    c['c_ssdmaskneg'] = np.where(ql >= kl, 0.0, NEG).astype(np.float32)
    c['c_negtri'] = np.where(kl >= ql, -1.0, 0.0).astype(np.float32)
    sel = np.zeros((16, 16 * 128), np.float32)
    for k in range(16):
        sel[k, k * 128:(k + 1) * 128] = 1.0
    c['c_sel'] = sel
    blk = np.zeros((128, 128), np.float32)
    blk[:64, :64] = 1.0; blk[64:, 64:] = 1.0
    c['c_ones_blk'] = blk
    return c


CONST_SHAPES = {k: v.shape for k, v in host_consts().items()}

IN_SHAPES = dict(
    x_prompt=(4, SEQ, D), x_sample=(2, SSEQ, D),
    cache_da_k=(DEPTH, 2, PAST, 512), cache_da_v=(DEPTH, 2, PAST, 512),
    cache_sb_k=(DEPTH, 2, PAST, 512), cache_sb_v=(DEPTH, 2, PAST, 512),
    state_ssm=(DEPTH, 2, 8, 64, 128), state_ssm_conv=(DEPTH, 2, 3, 1024), state_ffn_conv=(DEPTH, 2, 2, DFF),
    meta_tokens=(NMETA, D), rel_bias_table=(32, 4), w_in=(DEPTH, D, DIN), w_out=(DEPTH, DMIX, D),
    norm_mix=(DEPTH, D), norm_ffn=(DEPTH, D), da_lambda_q1=(DEPTH, 64), da_lambda_k1=(DEPTH, 64),
    da_lambda_q2=(DEPTH, 64), da_lambda_k2=(DEPTH, 64), da_subln=(DEPTH, 128),
    ssm_conv_w=(DEPTH, 4, 1024), ssm_conv_b=(DEPTH, 1024), ssm_dt_bias=(DEPTH, 8), ssm_a_log=(DEPTH, 8),
    ssm_d=(DEPTH, 8), ssm_norm=(DEPTH, 512), ffn_w_gate=(DEPTH, D, DFF), ffn_w_up=(DEPTH, D, DFF),
    ffn_w_down=(DEPTH, DFF, D), ffn_conv_w=(DEPTH, 3, DFF), ffn_conv_b=(DEPTH, DFF), final_norm=(D,),
)
OUT_SHAPES = dict(
    y_prompt=(4, SEQ, D), y_sample=(2, SSEQ, D),
    p_da_k=(DEPTH, 4, LP, 512), p_da_v=(DEPTH, 4, LP, 512), p_sb_k=(DEPTH, 4, LP, 512), p_sb_v=(DEPTH, 4, LP, 512),
    p_ssm=(DEPTH, 4, 512, 128), p_ssm_conv=(DEPTH, 4, 3, 1024), p_ffn_conv=(DEPTH, 4, 2, DFF),
    s_da_k=(DEPTH, 2, SSEQ, 512), s_da_v=(DEPTH, 2, SSEQ, 512), s_sb_k=(DEPTH, 2, SSEQ, 512), s_sb_v=(DEPTH, 2, SSEQ, 512),
    s_ssm=(DEPTH, 2, 512, 128), s_ssm_conv=(DEPTH, 2, 3, 1024), s_ffn_conv=(DEPTH, 2, 2, DFF),
)
OUT_ORDER = ['y_prompt', 'y_sample', 'p_da_k', 'p_da_v', 'p_sb_k', 'p_sb_v', 'p_ssm', 'p_ssm_conv', 'p_ffn_conv',
             's_da_k', 's_da_v', 's_sb_k', 's_sb_v', 's_ssm', 's_ssm_conv', 's_ffn_conv']


PHASES = ("da", "ssd", "sb", "ffn")


def build(units, n_layers=DEPTH, phases=PHASES, dbg=9):
    nc = bass.Bass("TRN2", target_bir_lowering=False)
    dr = {}
    for k, shp in IN_SHAPES.items():
        dr[k] = nc.dram_tensor(k, list(shp), F32, kind="ExternalInput").ap()
    for k, shp in CONST_SHAPES.items():
        dr[k] = nc.dram_tensor(k, list(shp), F32, kind="ExternalInput").ap()
    for k, shp in OUT_SHAPES.items():
        dr[k] = nc.dram_tensor(k, list(shp), F32, kind="ExternalOutput").ap()
    tvs = nc.dram_tensor("tvec_scratch", [4, 512], F32, kind="Internal").ap()

    P = Prog()
    es = ExitStack()
    with es:
        def sb(name, shape, dt=F32):
            return es.enter_context(nc.sbuf_tensor(name, shape, dt))

        xT = sb("xT", [128, 8, NT]); hT = sb("hT", [128, 8, NT], BF16); actT = sb("actT", [128, 4, NT], BF16)
        SL = sb("SL", [128, 4, NT])
        SLb = [SL[:, i, :].bitcast(BF16) for i in range(4)]

        def slot(i):
            return SLb[i // 2][:, (i % 2) * NT:(i % 2 + 1) * NT]
        VA, VB0, VB1 = [SLb[r][:, 0:17 * 128].rearrange("p (b n) -> p b n", n=128) for r in (1, 2, 3)]
        ksn = sb("ksn", [128, 16], BF16); vsn = sb("vsn", [16, 128], BF16)
        wst = [sb("wst%d" % i, [128, 8, 128]) for i in range(1)]
        wbf = [sb("wbf%d" % i, [128, 8, 128], BF16) for i in range(2)]
        stg = [sb("stg%d" % i, [128, 512]) for i in range(2)]
        kvo = [sb("kvo%d" % i, [128, 128]) for i in range(2)]
        cvp = sb("cvp", [128, NT + 4]); cvs = sb("cvs", [128, 3 + SSEQ])
        wk = {}
        for nm, shp, dt in [("wa", [128, 512], F32), ("wb", [128, 512], F32), ("wc", [128, 512], F32), ("wd", [128, 512], F32),
                            ("ba", [128, 512], BF16), ("bb", [128, 512], BF16), ("bc", [128, 512], BF16), ("bd", [128, 512], BF16)]:
            wk[nm] = sb(nm, shp, dt)
        identf = sb("identf", [128, 128]); identb = sb("identb", [128, 128], BF16); antiid = wk["wb"][:, 0:128]
        onesb = sb("onesb", [128, 128], BF16); negones = sb("negones", [128, 128], BF16); negtri = sb("negtri", [128, 128], BF16)
        onesblk = sb("onesblk", [128, 128], BF16)
        damask = wk["wb"][:, 128:256]; sbmask01 = sb("sbmask01", [128, 128]); sbmaskneg = sb("sbmaskneg", [128, 128])
        ssdmaskneg = sb("ssdmaskneg", [128, 128]); selT = sb("selT", [64, 8, 128])
        BT0 = sb("BT0", [128, 4, 128], BF16); BTm1 = sb("BTm1", [128, 4, 128], BF16); BTme = sb("BTme", [128, 4, 128], BF16); CH = sb("CH", [128, 4, 128], BF16); sbmasknegb = sb("sbmasknegb", [128, 128], BF16)
        chcol = sb("chcol", [128, 4]); tab = sb("tab", [32, 4])
        hank = wk["wd"][:, 256:384]
        gmix = sb("gmix", [128, 8]); gffn = sb("gffn", [128, 8]); gfin = sb("gfin", [128, 8])
        lamv = wk["wd"][:, 0:256].rearrange("p (a b) -> p a b", b=64); lamw = sb("lamw", [128, 8]); neglam = sb("neglam", [128, 1]); gsub = sb("gsub", [128, 1])
        cw = sb("cw", [128, 8, 4]); cb = sb("cb", [128, 8]); dtb = sb("dtb", [8, 4]); dcol = sb("dcol", [128, 4]); nw = sb("nw", [128, 4])
        fcw = sb("fcw", [128, 22, 3]); fcb = sb("fcb", [128, 22]); prm = sb("prm", [8, 512])
        dcT = cvp[0:64, 0:NT]; ones8 = sb("ones8", [8, 128]); dctm = sb("dctm", [128, 64]); negcs = sb("negcs", [128, 8])
        xdtp = sb("xdtp", [128, 8, 128], BF16); hbfp = sb("hbfp", [128, 8, 128], BF16); hst = sb("hst", [128, 8, 64])
        btm = sb("btm", [128, 2, 128], BF16); erow = wk["wc"][:, 256:384]; ltb = wk["wc"][:, 384:512]; mtb = sb("mtb", [128, 128], BF16)
        cst = sb("cst", [128, 128], BF16); xdd = sb("xdd", [128, 64], BF16); yv = wk["wb"][:, :].rearrange("p (c n) -> p c n", n=128); szb = wk["wa"][:, :].rearrange("p (c n) -> p c n", n=128)
        sqb = wk["ba"][:, :].rearrange("p (c n) -> p c n", n=128);
        ps = [es.enter_context(nc.psum_tensor("ps%d" % i, [128, 512], F32)) for i in range(8)]

        def pk(b):
            return "ps%d" % b

        def MM(out, lhsT, rhs, start, stop, r, w):
            P.op('pe', lambda e: e.matmul(out, lhsT=lhsT, rhs=rhs, start=start, stop=stop), r, w)

        def TR(out, in_, ident, r, w):
            P.op('pe', lambda e: e.transpose(out, in_, ident), r, w)

        def ACT(out, in_, func, r, w, bias=None, scale=None, accum_out=None):
            kw = {}
            if bias is not None: kw['bias'] = bias
            if scale is not None: kw['scale'] = scale
            if accum_out is not None: kw['accum_out'] = accum_out
            P.op('act', lambda e: e.activation(out=out, in_=in_, func=func, **kw), r, w)

        def V(eng, name, r, w, **kw):
            P.op(eng, lambda e: getattr(e, name)(**kw), r, w)

        def DMA(out, in_, r, w, stream, eng='sp', slow=False):
            if slow:
                P.op(eng, lambda e: e.dma_start(out=out, in_=in_, allow_slow_non_contiguous=True), r, w, stream=stream)
            else:
                P.op(eng, lambda e: e.dma_start(out=out, in_=in_), r, w, stream=stream)

        rot = {}

        def nxt(name, n):
            v = rot.get(name, 0)
            rot[name] = (v + 1) % n
            return v

        GR = [(0, 32)] + [(32 + 512 * i, 512) for i in range(4)]

        def grp_of_col(c):
            return 0 if c < 32 else (c - 32) // 512 + 1

        cload = [("c_ident", identf), ("c_sbmask01", sbmask01),
                 ("c_sbmaskneg", sbmaskneg), ("c_ssdmaskneg", ssdmaskneg)]
        for nm, t in cload:
            DMA(t[:], dr[nm][:, :], [], [nm], stream="c_" + nm)
        DMA(antiid, dr["c_antiid"][:, :], [], ["wb"], stream="c_wb")
        DMA(damask, dr["c_damask"][:, :], ["wb"], ["wb"], stream="c_wb")
        DMA(tab[:], dr["rel_bias_table"][:, :], [], ["tab"], stream="c_tab")
        DMA(wk["wa"][:, 0:128], dr["c_negtri"][:, :], [], ["wa"], stream="c_wa")
        V('dve', 'tensor_copy', ["wa"], ["negtri"], out=negtri[:], in_=wk["wa"][:, 0:128])
        DMA(wk["wa"][:, 128:256], dr["c_ones_blk"][:, :], ["wa"], ["wa"], stream="c_wa")
        V('dve', 'tensor_copy', ["wa"], ["onesblk"], out=onesblk[:], in_=wk["wa"][:, 128:256])
        V('dve', 'tensor_copy', ["c_ident"], ["identb"], out=identb[:], in_=identf[:])
        V('dve', 'tensor_copy', ["c_sbmaskneg"], ["sbmasknegb"], out=sbmasknegb[:], in_=sbmaskneg[:])
        V('pool', 'memset', [], ["onesb"], ap=onesb[:], constant=1.0)
        V('pool', 'memset', [], ["negones"], ap=negones[:], constant=-1.0)
        V('pool', 'memset', [], ["ones8"], ap=ones8[:], constant=1.0)
        V('pool', 'memset', [], ["cvp"], ap=cvp[:], constant=0.0)
        V('pool', 'memset', [], ["cvs"], ap=cvs[:], constant=0.0)
        V('pool', 'memset', [], ["xdtp"], ap=xdtp[:], constant=0.0)
        V('pool', 'memset', [], ["hbfp"], ap=hbfp[:], constant=0.0)
        V('pool', 'memset', [], ["xTall"] + ["xT%d_%d" % (c, g) for c in range(8) for g in range(5)], ap=xT[:], constant=0.0)
        for i in range(4):
            V('pool', 'memset', [], ["SL%d_%d" % (i, g) for g in range(5)], ap=SL[:, i, :], constant=0.0)
        V('pool', 'memset', [], ["selT"], ap=selT[:], constant=0.0)
        DMA(selT[32:40, :, :], dr["c_sel"][0:8, 0:1024].rearrange("k (h m) -> k h m", h=8), ["selT"], ["selT"], stream="c_sel")
        DMA(wk["wc"][0:32, :], dr["c_eh"][:, :], [], ["wc"], stream="c_eh")
        MM(ps[0][0:4, :], tab[:], wk["wc"][0:32, :], True, True, ["tab", "wc"], [pk(0)])
        V('dve', 'tensor_copy', [pk(0)], ["wd"], out=wk["wd"][0:4, :], in_=ps[0][0:4, :])
        DMA(tvs[:, :], wk["wd"][0:4, :], ["wd"], ["tvs"], stream="c_tvs")
        from concourse.ap import AP as _AP
        for h in range(4):
            for ti, (dst, off, msk) in enumerate([(BT0, 128, True), (BTm1, 0, False), (BTme, 112, False)]):
                src = _AP(tvs.tensor, tvs[h, off:off + 1].offset, [[1, 128], [1, 128]])
                DMA(hank, src, ["tvs"], ["wd"], stream="c_hank")
                MM(ps[1][:, 0:128], hank, antiid, True, True, ["wd", "wb"], [pk(1)])
                if msk:
                    V('dve', 'scalar_tensor_tensor', [pk(1), "wb"], ["BT"], out=dst[:, h, :], in0=ps[1][:, 0:128], scalar=8.0, in1=damask, op0=ALU.mult, op1=ALU.add)
                else:
                    V('dve', 'tensor_scalar', [pk(1)], ["BT"], out=dst[:, h, :], in0=ps[1][:, 0:128], scalar1=8.0, scalar2=None, op0=ALU.mult)
            src = _AP(tvs.tensor, tvs[h, 0:1].offset, [[0, 128], [1, 1]])
            DMA(chcol[:, h:h + 1], src, ["tvs"], ["chcol"], stream="c_chc")
            ACT(CH[:, h, :], identf[:], AF.Identity, ["chcol", "c_ident"], ["BT"], bias=chcol[:, h:h + 1], scale=0.0)
            V('dve', 'tensor_scalar', ["BT"], ["BT"], out=CH[:, h, :], in0=CH[:, h, :], scalar1=8.0, scalar2=None, op0=ALU.mult)

        def load_T(src2d, r, nch, dst3, keys_w, width=128):
            c = 0
            while c < nch:
                m = min(nch - c, 4)
                DMA(prm[0:r, 0:m * width], src2d[:, c * width:(c + m) * width], ["prm"], ["prm"], stream="prm")
                for j in range(m):
                    TR(ps[0][0:width, j * r:(j + 1) * r], prm[0:r, j * width:(j + 1) * width], identf[0:r, 0:r], ["prm", "c_ident"], [pk(0)])
                V('dve', 'tensor_copy', [pk(0)], keys_w, out=dst3[:, c:c + m, :], in_=ps[0][0:width, 0:m * r].rearrange("p (j r) -> p j r", r=r))
                c += m
        load_T(dr["final_norm"].rearrange("(o n) -> o n", o=1), 1, 8, gfin[:].rearrange("p (c o) -> p c o", o=1), ["gfin"])

        def load_w(src3):
            nch, ncols = src3.shape[1], src3.shape[2]
            i = nxt("wst", 1); j = nxt("wbf", 2)
            DMA(wst[i][:, 0:nch, 0:ncols], src3, [], ["wst%d" % i], stream="wst%d" % i)
            V('pool', 'tensor_copy', ["wst%d" % i], ["wbf%d" % j], out=wbf[j][:, 0:nch, 0:ncols], in_=wst[i][:, 0:nch, 0:ncols])
            return wbf[j][:, 0:nch, 0:ncols], "wbf%d" % j

        def wsrc(w2d, r0, nch, c0, ncols):
            return w2d[r0:r0 + nch * 128, c0:c0 + ncols].rearrange("(kc p) n -> p kc n", p=128)

        def hkeys(g):
            return ["hT_%d" % g]

        def proj_fm(wt, wkey, ncols, evac):
            for g, (a, n) in enumerate(GR):
                b = nxt("pj", 2)
                for kc in range(8):
                    MM(ps[b][0:ncols, 0:n], wt[:, kc, :], hT[:, kc, a:a + n], kc == 0, kc == 7, [wkey] + hkeys(g), [pk(b)])
                evac(g, ps[b][0:ncols, 0:n], pk(b))

        def proj_tm(wt, wkey, a, n, ncols):
            b = nxt("pj", 2)
            g = grp_of_col(a)
            for kc in range(8):
                MM(ps[b][0:n, 0:ncols], hT[:, kc, a:a + n], wt[:, kc, :], kc == 0, kc == 7, [wkey] + hkeys(g), [pk(b)])
            return ps[b][0:n, 0:ncols], pk(b)

        def rms_stats(g, src_ap_fn, src_keys, nchunks, denom):
            a, n = GR[g] if isinstance(g, int) else g
            b = nxt("pj", 2)
            for c in range(nchunks):
                t = wk["ba"] if c % 2 == 0 else wk["bb"]
                tk = "ba" if c % 2 == 0 else "bb"
                ACT(t[:, 0:n], src_ap_fn(c), AF.Square, src_keys(c), [tk])
                MM(ps[b][:, 0:n], onesb[:], t[:, 0:n], c == 0, c == nchunks - 1, [tk, "onesb"], [pk(b)])
            ACT(wk["wc"][:, 0:n], ps[b][:, 0:n], AF.Sqrt, [pk(b)], ["wc"], bias=EPS_AP[:, 0:1], scale=1.0 / denom)
            V('dve', 'reciprocal', ["wc"], ["wd"], out=wk["wd"][:, 0:n], in_=wk["wc"][:, 0:n])
            return wk["wd"][:, 0:n]

        EPS_AP = sb("epsap", [128, 1])
        V('pool', 'memset', [], ["epsap"], ap=EPS_AP[:], constant=EPS)

        def rmsnorm_to_hT(gw):
            for g, (a, n) in enumerate(GR):
                rinv = rms_stats(g, lambda c: xT[:, c, a:a + n], lambda c: ["xT%d_%d" % (c, g)], 8, float(D))
                for c in range(8):
                    V('dve', 'scalar_tensor_tensor', ["xT%d_%d" % (c, g), "wd", "gw"], hkeys(g), out=hT[:, c, a:a + n], in0=xT[:, c, a:a + n],
                      scalar=gw[:, c:c + 1], in1=rinv, op0=ALU.mult, op1=ALU.mult)

        def p_blocks():
            return [(0, 16, 16)] + [(16 + 128 * i, 128, 32 + 128 * i) for i in range(16)]

        def s_blocks():
            return [(128 * i, 128, None) for i in range(16)] + [(2048, 16, 0)]

        P_QT = [[0], [1, 2, 3, 4], [5, 6, 7, 8], [9, 10, 11, 12], [13, 14, 15, 16]]
        S_QT = [[16]]

        def kgrp(kpos):
            return grp_of_col(kpos + 16)

        st = dict(layer=0, pi=0, si=None)

        def kv_out(wt, wkey, ncols, blocks_cols, out_fn, bf_dst_fn):
            for bi, a, n in blocks_cols:
                pt, pkey = proj_tm(wt, wkey, a, n, ncols)
                i = nxt("kvo", 2)
                V('dve', 'tensor_copy', [pkey], ["kvo%d" % i], out=kvo[i][0:n, 0:ncols], in_=pt)
                DMA(out_fn(bi), kvo[i][0:n, 0:ncols], ["kvo%d" % i], [], stream="kvo%d" % i, eng='pool')
                if bf_dst_fn is not None:
                    for dst, src, keys in bf_dst_fn(bi, pt):
                        ACT(dst, src, AF.Copy, [pkey], keys)

        def tok_blocks(has_s):
            l = [("p", bi, col, n) for bi, (kp, n, col) in enumerate(p_blocks())]
            if has_s:
                l.append(("s", 16, 0, 16))
            return l

        def load_cache(ck, cv, l, s, c0, ktslot, vdsts):
            for q in range(4):
                i = nxt("stg", 2)
                src = dr[ck][l, s, q * 512:(q + 1) * 512, c0:c0 + 128].rearrange("(b p) n -> p b n", p=128)
                DMA(stg[i][:, 0:512].rearrange("p (b n) -> p b n", n=128), src, [], ["stg%d" % i], stream="stg%d" % i)
                b = nxt("pj", 2)
                for j in range(4):
                    TR(ps[b][:, j * 128:(j + 1) * 128], stg[i][:, j * 128:(j + 1) * 128], identf[:], ["stg%d" % i, "c_ident"], [pk(b)])
                kp = q * 512
                V('dve', 'tensor_copy', [pk(b)], ["SL%d_%d" % (ktslot // 2, kgrp(kp)), "SL%d_%d" % (ktslot // 2, kgrp(kp + 511))],
                  out=slot(ktslot)[:, kp:kp + 512], in_=ps[b][:, :])
                i = nxt("stg", 2)
                src = dr[cv][l, s, q * 512:(q + 1) * 512, c0:c0 + 128].rearrange("(b p) n -> p b n", p=128)
                DMA(stg[i][:, 0:512].rearrange("p (b n) -> p b n", n=128), src, [], ["stg%d" % i], stream="stg%d" % i)
                for (vd, vkey, ca, cn) in vdsts:
                    V('pool', 'tensor_copy', ["stg%d" % i], [vkey], out=vd[:, 4 * q:4 * q + 4, ca:ca + cn],
                      in_=stg[i][:, 0:512].rearrange("p (b n) -> p b n", n=128)[:, :, ca:ca + cn])

        def load_layer_params(l):
            load_T(dr["norm_mix"][l:l + 1, :], 1, 8, gmix[:].rearrange("p (c o) -> p c o", o=1), ["gw"])
            load_T(dr["norm_ffn"][l:l + 1, :], 1, 8, gffn[:].rearrange("p (c o) -> p c o", o=1), ["gw2"])
            load_T(dr["ssm_conv_w"][l, :, :], 4, 8, cw[:], ["cw"])
            load_T(dr["ssm_conv_b"][l:l + 1, :], 1, 8, cb[:].rearrange("p (c o) -> p c o", o=1), ["cw"])
            load_T(dr["ssm_norm"][l:l + 1, :], 1, 4, nw[:].rearrange("p (c o) -> p c o", o=1), ["cw"])
            load_T(dr["ffn_conv_w"][l, :, :], 3, 22, fcw[:], ["fcw"])
            load_T(dr["ffn_conv_b"][l:l + 1, :], 1, 22, fcb[:].rearrange("p (c o) -> p c o", o=1), ["fcw"])
            lam_init = 0.8 - 0.6 * math.exp(-0.3 * l)
            load_T(dr["da_subln"][l:l + 1, :], 1, 1, lamw[:, 0:1].rearrange("p (c o) -> p c o", o=1), ["lamw"])
            V('dve', 'tensor_scalar', ["lamw"], ["gsub"], out=gsub[:], in0=lamw[:, 0:1], scalar1=(1.0 - lam_init), scalar2=None, op0=ALU.mult)
            for i, nm in enumerate(["da_lambda_q1", "da_lambda_k1", "da_lambda_q2", "da_lambda_k2"]):
                DMA(lamv[:, i, :], dr[nm][l:l + 1, :].partition_broadcast(128), [], ["wd"], stream="wd")
            for j in range(2):
                V('dve', 'tensor_tensor', ["wd"], ["lamt"], out=wk["wa"][:, 0:64], in0=lamv[:, 2 * j, :], in1=lamv[:, 2 * j + 1, :], op=ALU.mult)
                V('dve', 'reduce_sum', ["lamt"], ["lamw"], out=lamw[:, 2 + j:3 + j], in_=wk["wa"][:, 0:64], axis=AX.X)
            ACT(lamw[:, 4:6], lamw[:, 2:4], AF.Exp, ["lamw"], ["lamw"])
            V('dve', 'tensor_tensor', ["lamw"], ["lamw"], out=lamw[:, 6:7], in0=lamw[:, 5:6], in1=lamw[:, 4:5], op=ALU.subtract)
            V('dve', 'tensor_scalar', ["lamw"], ["neglam"], out=neglam[:], in0=lamw[:, 6:7], scalar1=-lam_init, scalar2=None, op0=ALU.add)
            DMA(dtb[:, 0:1], dr["ssm_dt_bias"][l:l + 1, :].rearrange("o h -> h o"), [], ["dtb"], stream="dtb", slow=True)
            DMA(dtb[:, 1:2], dr["ssm_a_log"][l:l + 1, :].rearrange("o h -> h o"), ["dtb"], ["dtb"], stream="dtb", slow=True)
            ACT(dtb[:, 2:3], dtb[:, 1:2], AF.Exp, ["dtb"], ["dtb"])
            V('dve', 'tensor_scalar', ["dtb"], ["dtb"], out=dtb[:, 3:4], in0=dtb[:, 2:3], scalar1=-1.0, scalar2=None, op0=ALU.mult)
            for c in range(4):
                for hh in range(2):
                    DMA(dcol[64 * hh:64 * hh + 64, c:c + 1], dr["ssm_d"][l:l + 1, 2 * c + hh:2 * c + hh + 1].partition_broadcast(64), [], ["dcol"], stream="dcol")

        def da_bias(kind, kb, qb, h):
            if kind == 'p':
                if qb == kb:
                    n = 16 if kb == 0 else 128
                    return BT0[0:n, h, 0:n]
                if qb == kb + 1:
                    return BTme[0:16, h, :] if kb == 0 else BTm1[:, h, :]
                return None
            else:
                if kb == 16:
                    return BT0[0:16, h, 0:16]
                if kb == 15:
                    return BTm1[:, h, 0:16]
                return None

        def attn_da(kind, blocks, qtiles, h, mixc):
            KT = slot(1); QT = slot(0)
            for qt in qtiles:
                qcols = [blocks[b][2] for b in qt]; qa = qcols[0]
                n = sum(blocks[b][1] for b in qt)
                qg = grp_of_col(qa)
                offs = {}
                o = 0
                for b in qt:
                    offs[b] = o; o += blocks[b][1]
                kmax = qt[-1]
                for kb in range(kmax + 1):
                    kp, ksz, _ = blocks[kb]
                    vq = [b for b in qt if b >= kb]
                    off = offs[vq[0]]
                    band = any(da_bias(kind, kb, b, h) is not None for b in vq)
                    for m in range(2):
                        zb = 2 + nxt("daz", 2)
                        kkeys = ["SL0_%d" % kgrp(kp), "SL0_%d" % qg]
                        MM(ps[zb][0:ksz, off:n], KT[64 * m:64 * m + 64, kp:kp + ksz], QT[64 * m:64 * m + 64, qa + off:qa + n], True, not band, kkeys, [pk(zb)])
                        if band:
                            for bi, b in enumerate(vq):
                                t = da_bias(kind, kb, b, h)
                                qsz = blocks[b][1]
                                if t is None:
                                    t = CH[0:ksz, h, 0:qsz]
                                MM(ps[zb][0:ksz, offs[b]:offs[b] + qsz], identb[0:ksz, 0:ksz], t, False, bi == len(vq) - 1, ["identb", "BT"], [pk(zb)])
                        ei = nxt("dae", 2)
                        et = wk["ba"] if ei == 0 else wk["bb"]; ek = "ba" if ei == 0 else "bb"
                        if band:
                            ACT(et[0:ksz, off:n], ps[zb][0:ksz, off:n], AF.Exp, [pk(zb)], [ek], scale=0.125)
                        else:
                            ACT(et[0:ksz, off:n], ps[zb][0:ksz, off:n], AF.Exp, [pk(zb), "chcol"], [ek], scale=0.125, bias=chcol[0:ksz, h:h + 1])
                        MM(ps[4 + m][:, off:n], VA[0:ksz, kb, :], et[0:ksz, off:n], kb == 0, kb == kmax, [ek, "VA"], [pk(4 + m)])
                        MM(ps[6 + m][:, off:n], onesb[0:ksz, :], et[0:ksz, off:n], kb == 0, kb == kmax, [ek, "onesb"], [pk(6 + m)])
                V('dve', 'reciprocal', [pk(6)], ["wa"], out=wk["wa"][:, 0:n], in_=ps[6][:, 0:n])
                V('dve', 'reciprocal', [pk(7)], ["wb"], out=wk["wb"][:, 0:n], in_=ps[7][:, 0:n])
                V('dve', 'tensor_tensor', [pk(4), "wa"], ["wa"], out=wk["wa"][:, 0:n], in0=ps[4][:, 0:n], in1=wk["wa"][:, 0:n], op=ALU.mult)
                V('dve', 'tensor_tensor', [pk(5), "wb"], ["wb"], out=wk["wb"][:, 0:n], in0=ps[5][:, 0:n], in1=wk["wb"][:, 0:n], op=ALU.mult)
                V('dve', 'scalar_tensor_tensor', ["wa", "wb", "neglam"], ["wa"], out=wk["wa"][:, 0:n], in0=wk["wb"][:, 0:n], scalar=neglam[:, 0:1],
                  in1=wk["wa"][:, 0:n], op0=ALU.mult, op1=ALU.add)
                rinv = rms_stats((qa, n), lambda c: wk["wa"][:, 0:n], lambda c: ["wa"], 1, 128.0)
                V('dve', 'scalar_tensor_tensor', ["wa", "wd", "gsub"], ["act%d_%d" % (mixc, qg)], out=actT[:, mixc, qa:qa + n], in0=wk["wa"][:, 0:n],
                  scalar=gsub[:, 0:1], in1=rinv, op0=ALU.mult, op1=ALU.mult)

        def kt_evac_split(dst_slot_idx, scale=None):
            KTd = slot(dst_slot_idx)
            si = dst_slot_idx // 2

            def f(g, pap, pkey):
                a, n = GR[g]
                if g == 0:
                    V('dve', 'tensor_copy', [pkey], ["ksn"], out=ksn[:, 0:16], in_=pap[:, 0:16])
                    V('dve', 'tensor_copy', [pkey], ["SL%d_0" % si], out=KTd[:, 0:16], in_=pap[:, 16:32])
                else:
                    V('dve', 'tensor_copy', [pkey], ["SL%d_%d" % (si, g)], out=KTd[:, a - 16:a - 16 + n], in_=pap)
            return f

        def qt_evac(scale):
            def f(g, pap, pkey):
                a, n = GR[g]
                if scale is None:
                    V('dve', 'tensor_copy', [pkey], ["SL0_%d" % g], out=slot(0)[:, a:a + n], in_=pap)
                else:
                    V('dve', 'tensor_scalar', [pkey], ["SL0_%d" % g], out=slot(0)[:, a:a + n], in0=pap, scalar1=scale, scalar2=None, op0=ALU.mult)
            return f

        def attn_head_group(l, pi, si, kind_da, hidx):
            if kind_da:
                qc, kc_, vc = hidx * 128, OFF_DA_K + hidx * 128, OFF_DA_V + hidx * 128
                okp, ovp, oks, ovs, ck, cv = "p_da_k", "p_da_v", "s_da_k", "s_da_v", "cache_da_k", "cache_da_v"
            else:
                qc, kc_, vc = OFF_SB_Q + hidx * 128, OFF_SB_K + hidx * 128, OFF_SB_V + hidx * 128
                okp, ovp, oks, ovs, ck, cv = "p_sb_k", "p_sb_v", "s_sb_k", "s_sb_v", "cache_sb_k", "cache_sb_v"
            c0 = hidx * 128
            w2 = dr["w_in"][l]
            wt, wkey = load_w(wsrc(w2, 0, 8, qc, 128))
            proj_fm(wt, wkey, 128, qt_evac(None if kind_da else 0.125))
            wt, wkey = load_w(wsrc(w2, 0, 8, kc_, 128))
            proj_fm(wt, wkey, 128, kt_evac_split(1))
            tb = tok_blocks(si is not None)
            pb = p_blocks()

            def kout(item):
                kind, bi, col, n = item
                if kind == 'p':
                    return dr[okp][l, pi, pb[bi][0]:pb[bi][0] + n, c0:c0 + 128]
                return dr[oks][l, si, :, c0:c0 + 128]

            def vout(item):
                kind, bi, col, n = item
                if kind == 'p':
                    return dr[ovp][l, pi, pb[bi][0]:pb[bi][0] + n, c0:c0 + 128]
                return dr[ovs][l, si, :, c0:c0 + 128]
            kv_out(wt, wkey, 128, [(it, it[2], it[3]) for it in tb], kout, None)
            wt, wkey = load_w(wsrc(w2, 0, 8, vc, 128))

            def vbf(item, pt):
                kind, bi, col, n = item
                if kind == 's':
                    return [(vsn[0:16, :], pt, ["vsn"])]
                if kind_da:
                    return [(VA[0:n, bi, :], pt, ["VA"])]
                return [(VB0[0:n, bi, 0:64], pt[:, 0:64], ["VB0"]), (VB1[0:n, bi, 64:128], pt[:, 64:128], ["VB1"])]
            kv_out(wt, wkey, 128, [(it, it[2], it[3]) for it in tb], vout, vbf)
            mixc = hidx
            if dbg >= 2:
                if kind_da:
                    attn_da('p', pb, P_QT, hidx, mixc)
                else:
                    attn_sb(pb, P_QT, mixc)
            if si is not None and dbg >= 3:
                vd = [(VA, "VA", 0, 128)] if kind_da else [(VB0, "VB0", 0, 64), (VB1, "VB1", 64, 64)]
                load_cache(ck, cv, l, si, c0, 1, vd)
                V('dve', 'tensor_copy', ["ksn"], ["SL0_4"], out=slot(1)[:, 2048:2064], in_=ksn[:, 0:16])
                for (vdst, vkey, ca, cn) in vd:
                    V('pool', 'tensor_copy', ["vsn"], [vkey], out=vdst[0:16, 16, ca:ca + cn], in_=vsn[0:16, ca:ca + cn])
                if kind_da:
                    attn_da('s', s_blocks(), S_QT, hidx, mixc)
                else:
                    attn_sb(s_blocks(), S_QT, mixc)

        def attn_sb(blocks, qtiles, mixc):
            KT = slot(1); QT = slot(0)
            for qt in qtiles:
                qa = blocks[qt[0]][2]
                n = sum(blocks[b][1] for b in qt)
                qg = grp_of_col(qa)
                offs = {}
                o = 0
                for b in qt:
                    offs[b] = o; o += blocks[b][1]
                kmax = qt[-1]
                V('dve', 'memset', [], [pk(7)], ap=ps[7][:, 0:n], constant=0.0)
                for hh in range(2):
                    Vp = VB0 if hh == 0 else VB1; vkey = "VB0" if hh == 0 else "VB1"
                    pr = slice(64 * hh, 64 * hh + 64)
                    for kb in range(kmax, -1, -1):
                        kp, ksz, _ = blocks[kb]
                        vq = [b for b in qt if b >= kb]
                        off = offs[vq[0]]
                        diag = (vq[0] == kb)
                        dsz = blocks[kb][1] if diag else 0
                        off2 = off + dsz
                        kkeys = ["SL0_%d" % kgrp(kp), "SL0_%d" % qg]
                        zb = 2 + nxt("sbz", 2)
                        MM(ps[zb][0:ksz, off:n], KT[pr, kp:kp + ksz], QT[pr, qa + off:qa + n], True, True, kkeys, [pk(zb)])
                        ACT(wk["wa"][0:ksz, off:n], ps[zb][0:ksz, off:n], AF.Exp, [pk(zb)], ["wa"])
                        si_ = nxt("sbsp", 2)
                        sp = wk["ba"] if si_ == 0 else wk["bb"]; spk = "ba" if si_ == 0 else "bb"
                        ACT(sp[0:ksz, off:n], wk["wa"][0:ksz, off:n], AF.Ln, ["wa"], [spk], bias=1.0)
                        if diag:
                            V('pool', 'tensor_tensor', [spk, "c_sbmask01"], [spk], out=sp[0:ksz, off:off2], in0=sp[0:ksz, off:off2], in1=sbmask01[0:ksz, 0:dsz], op=ALU.mult)
                        tb_ = 4 + nxt("sbt", 2)
                        MM(ps[tb_][0:ksz, off:n], KT[pr, kp:kp + ksz], QT[pr, qa + off:qa + n], True, False, kkeys, [pk(tb_)])
                        MM(ps[tb_][0:ksz, off:n], negtri[0:ksz, 0:ksz], sp[0:ksz, off:n], False, not diag, [spk, "negtri"], [pk(tb_)])
                        if diag:
                            MM(ps[tb_][0:ksz, off:off2], identb[0:ksz, 0:ksz], sbmasknegb[0:ksz, 0:dsz], False, True, ["identb", "sbmasknegb"], [pk(tb_)])
                        wi = nxt("sbw", 2)
                        wt_ = wk["bc"] if wi == 0 else wk["bd"]; wtk = "bc" if wi == 0 else "bd"
                        if diag:
                            ACT(wt_[0:ksz, off:off2], ps[tb_][0:ksz, off:off2], AF.Exp, [pk(tb_)], [wtk])
                        if off2 < n:
                            V('dve', 'tensor_tensor', [pk(tb_), "wc"], ["wb"], out=wk["wb"][0:ksz, off2:n], in0=ps[tb_][0:ksz, off2:n], in1=wk["wc"][0:ksz, off2:n], op=ALU.add)
                            ACT(wt_[0:ksz, off2:n], wk["wb"][0:ksz, off2:n], AF.Exp, ["wb"], [wtk])
                        if kb > 0:
                            MM(ps[6][:, off:n], negones[0:ksz, :], sp[0:ksz, off:n], True, True, [spk, "negones"], [pk(6)])
                            if diag:
                                V('dve', 'tensor_copy', [pk(6)], ["wc"], out=wk["wc"][:, off:off2], in_=ps[6][:, off:off2])
                            if off2 < n:
                                V('dve', 'tensor_tensor', [pk(6), "wc"], ["wc"], out=wk["wc"][:, off2:n], in0=ps[6][:, off2:n], in1=wk["wc"][:, off2:n], op=ALU.add)
                        first_h = (hh == 0)
                        if diag:
                            MM(ps[7][:, off:off2], Vp[0:ksz, kb, :], wt_[0:ksz, off:off2], False, False, [wtk, vkey], [pk(7)])
                        if off2 < n:
                            MM(ps[7][:, off2:n], Vp[0:ksz, kb, :], wt_[0:ksz, off2:n], False, (hh == 1 and kb == 0), [wtk, vkey], [pk(7)])
                ACT(actT[:, mixc, qa:qa + n], ps[7][:, 0:n], AF.Copy, [pk(7)], ["act%d_%d" % (mixc, qg)])

        def out_proj(l, mi):
            for c in range(8):
                wt, wkey = load_w(wsrc(dr["w_out"][l], mi * 512, 4, c * 128, 128))
                for g, (a, n) in enumerate(GR):
                    b = nxt("pj", 2)
                    for m in range(4):
                        MM(ps[b][:, 0:n], wt[:, m, :], actT[:, m, a:a + n], m == 0, m == 3, [wkey, "act%d_%d" % (m, g)], [pk(b)])
                    V('dve', 'tensor_tensor', [pk(b), "xT%d_%d" % (c, g)], ["xT%d_%d" % (c, g)], out=xT[:, c, a:a + n], in0=ps[b][:, 0:n], in1=xT[:, c, a:a + n], op=ALU.add)

        def ssd(l, pi, si):
            w2 = dr["w_in"][l]
            for c in range(4):
                wt, wkey = load_w(wsrc(w2, 0, 8, OFF_Z + c * 128, 128))

                def ev(g, pap, pkey, c=c):
                    a, n = GR[g]
                    V('dve', 'tensor_copy', [pkey], ["act%d_%d" % (c, g)], out=actT[:, c, a:a + n], in_=pap)
                proj_fm(wt, wkey, 128, ev)
            V('pool', 'memset', [], ["cvp"], ap=cvp[:, 0:3], constant=0.0)
            if si is not None:
                load_T(dr["state_ssm_conv"][l, si, :, :], 3, 8, wk["wa"][:, 0:24].rearrange("p (c r) -> p c r", r=3), ["wa"])
            for c in range(8):
                wt, wkey = load_w(wsrc(w2, 0, 8, OFF_XBC + c * 128, 128))

                def ev(g, pap, pkey):
                    a, n = GR[g]
                    if g == 0:
                        ACT(cvs[:, 3:19], pap[:, 0:16], AF.Copy, [pkey], ["cvs"])
                        ACT(cvp[:, 3:19], pap[:, 16:32], AF.Copy, [pkey], ["cvp"])
                    else:
                        ACT(cvp[:, 3 + a - 16:3 + a - 16 + n], pap, AF.Copy, [pkey], ["cvp"])
                proj_fm(wt, wkey, 128, ev)
                pt, pkey = proj_tm(wt, wkey, NT - 3, 3, 128)
                i = nxt("kvo", 2)
                V('dve', 'tensor_copy', [pkey], ["kvo%d" % i], out=kvo[i][0:3, :], in_=pt)
                DMA(dr["p_ssm_conv"][l, pi, :, c * 128:(c + 1) * 128], kvo[i][0:3, :], ["kvo%d" % i], [], stream="kvo%d" % i, eng='pool')
                if si is not None:
                    pt, pkey = proj_tm(wt, wkey, 13, 3, 128)
                    i = nxt("kvo", 2)
                    V('dve', 'tensor_copy', [pkey], ["kvo%d" % i], out=kvo[i][0:3, :], in_=pt)
                    DMA(dr["s_ssm_conv"][l, si, :, c * 128:(c + 1) * 128], kvo[i][0:3, :], ["kvo%d" % i], [], stream="kvo%d" % i, eng='pool')
                    V('dve', 'tensor_copy', ["wa"], ["cvs"], out=cvs[:, 0:3], in_=wk["wa"][:, 3 * c:3 * c + 3])
                srcs = [(cvp, LP, 16, "cvp")] + ([(cvs, SSEQ, 0, "cvs")] if si is not None else [])
                for (cv_, ln, col0, ckey) in srcs:
                    pos = 0
                    while pos < ln:
                        m = min(512, ln - pos)
                        acc = wk["wb"][:, 0:m]
                        V('dve', 'tensor_scalar', [ckey, "cw"], ["wb"], out=acc, in0=cv_[:, 3 + pos:3 + pos + m], scalar1=cw[:, c, 3:4], scalar2=cb[:, c:c + 1], op0=ALU.mult, op1=ALU.add)
                        for i in range(3):
                            V('dve', 'scalar_tensor_tensor', [ckey, "cw", "wb"], ["wb"], out=acc, in0=cv_[:, i + pos:i + pos + m], scalar=cw[:, c, i:i + 1], in1=acc, op0=ALU.mult, op1=ALU.add)
                        g0 = grp_of_col(col0 + pos); g1 = grp_of_col(col0 + pos + m - 1)
                        sl = c
                        ACT(slot(sl)[:, col0 + pos:col0 + pos + m], acc, AF.Silu, ["wb"], ["SL%d_%d" % (sl // 2, g) for g in range(g0, g1 + 1)])
                        pos += m
            V('pool', 'memset', [], ["cvp"], ap=cvp[0:64, :], constant=0.0)
            i_ = nxt("wst", 1); j_ = nxt("wbf", 2)
            DMA(wst[i_][:, 0:8, 0:8], wsrc(w2, 0, 8, OFF_DT, 8), [], ["wst%d" % i_], stream="wst%d" % i_)
            V('pool', 'tensor_copy', ["wst%d" % i_], ["wbf%d" % j_], out=wbf[j_][:, 0:8, 0:8], in_=wst[i_][:, 0:8, 0:8])

            def evdt(g, pap, pkey):
                a, n = GR[g]
                ACT(wk["wa"][0:8, 0:n], pap, AF.Exp, [pkey, "dtb"], ["wa"], bias=dtb[:, 0:1])
                ACT(dcT[0:8, a:a + n], wk["wa"][0:8, 0:n], AF.Ln, ["wa"], ["cvp"], bias=1.0)
            proj_fm(wbf[j_][:, 0:8, 0:8], "wbf%d" % j_, 8, evdt)
            subs = [('p', p_blocks(), list(range(17)))] + ([('s', s_blocks(), [16])] if si is not None else [])
            for kind, blocks, bl in subs:
                if kind == 'p':
                    V('pool', 'memset', [], ["hst"], ap=hst[:], constant=0.0)
                    V('pool', 'memset', [], ["hbfp"], ap=hbfp[:], constant=0.0)
                else:
                    for q in range(4):
                        i = nxt("stg", 2)
                        DMA(stg[i][:, 0:128], dr["state_ssm"][l, si].rearrange("h p n -> (h p) n")[q * 128:(q + 1) * 128, :], [], ["stg%d" % i], stream="stg%d" % i)
                        TR(ps[0][:, 0:128], stg[i][:, 0:128], identf[:], ["stg%d" % i, "c_ident"], [pk(0)])
                        V('dve', 'tensor_copy', [pk(0)], ["hst"], out=hst[:, 2 * q:2 * q + 2, :], in_=ps[0][:, 0:128].rearrange("p (h q) -> p h q", q=64))
                    for hd in range(8):
                        V('pool', 'tensor_copy', ["hst"], ["hbfp"], out=hbfp[:, hd, 64 * (hd % 2):64 * (hd % 2) + 64], in_=hst[:, hd, :])
                for b in bl:
                    kp, n, ca = blocks[b]
                    g = grp_of_col(ca)
                    V('dve', 'tensor_scalar', ["cvp", "dtb"], ["wa"], out=wk["wa"][0:8, 0:n], in0=dcT[0:8, ca:ca + n], scalar1=dtb[:, 3:4], scalar2=None, op0=ALU.mult)
                    V('dve', 'tensor_tensor_scan', ["wa", "ones8"], ["cvp"], out=dcT[32:40, ca:ca + n], data0=ones8[:, 0:n], data1=wk["wa"][0:8, 0:n], initial=0.0, op0=ALU.mult, op1=ALU.add)
                    TR(ps[0][0:n, 0:64], dcT[:, ca:ca + n], identf[0:64, 0:64], ["cvp", "c_ident"], [pk(0)])
                    V('dve', 'tensor_copy', [pk(0)], ["dctm"], out=dctm[0:n, :], in_=ps[0][0:n, 0:64])
                    V('dve', 'tensor_scalar', ["dctm"], ["negcs"], out=negcs[0:n, :], in0=dctm[0:n, 32:40], scalar1=-1.0, scalar2=None, op0=ALU.mult)
                    psb = ps[1][:, :].bitcast(BF16)
                    for c in range(4):
                        TR(psb[0:n, c * 128:(c + 1) * 128], slot(c)[:, ca:ca + n], identb[:], ["SL%d_%d" % (c // 2, g), "identb"], [pk(1)])
                    for hd in range(8):
                        V('dve', 'tensor_scalar', [pk(1), "dctm"], ["xdtp"], out=xdtp[0:n, hd, 64 * (hd % 2):64 * (hd % 2) + 64], in0=psb[0:n, hd * 64:(hd + 1) * 64],
                          scalar1=dctm[0:n, hd:hd + 1], scalar2=None, op0=ALU.mult)
                    psb2 = ps[0][:, :].bitcast(BF16)
                    for g2 in range(2):
                        TR(psb2[0:n, g2 * 128:(g2 + 1) * 128], slot(4 + g2)[:, ca:ca + n], identb[:], ["SL2_%d" % g, "identb"], [pk(0)])
                    V('dve', 'tensor_copy', [pk(0)], ["btm"], out=btm[0:n, :, :], in_=psb2[0:n, 0:256].rearrange("p (g m) -> p g m", m=128))
                    for g2 in range(2):
                        MM(ps[2][0:n, g2 * 128:g2 * 128 + n], slot(4 + g2)[:, ca:ca + n], slot(6 + g2)[:, ca:ca + n], True, True, ["SL2_%d" % g, "SL3_%d" % g], [pk(2)])
                    for hd in range(8):
                        g2 = hd // 4
                        MM(ps[3][:, 0:n], selT[:, hd, :], dcT[:, ca:ca + n], True, True, ["selT", "cvp"], [pk(3)])
                        MM(ps[3][0:n, 128:128 + n], selT[:, hd, 0:n], dcT[:, ca:ca + n], True, False, ["selT", "cvp"], [pk(3)])
                        MM(ps[3][0:n, 128:128 + n], identf[0:n, 0:n], ssdmaskneg[0:n, 0:n], False, True, ["c_ident", "c_ssdmaskneg"], [pk(3)])
                        ACT(erow[:, 0:n], ps[3][:, 0:n], AF.Exp, [pk(3)], ["wc"])
                        ACT(ltb[0:n, 0:n], ps[3][0:n, 128:128 + n], AF.Exp, [pk(3), "negcs"], ["wc"], bias=negcs[0:n, hd:hd + 1])
                        V('dve', 'tensor_tensor', ["wc", pk(2)], ["mtb"], out=mtb[0:n, 0:n], in0=ps[2][0:n, g2 * 128:g2 * 128 + n], in1=ltb[0:n, 0:n], op=ALU.mult)
                        V('pool', 'tensor_tensor', ["wc", "SL3_%d" % g], ["cst"], out=cst[:, 0:n], in0=slot(6 + g2)[:, ca:ca + n], in1=erow[:, 0:n], op=ALU.mult)
                        V('dve', 'tensor_scalar', ["xdtp", "wc"], ["xdd"], out=xdd[0:n, :], in0=xdtp[0:n, hd, 64 * (hd % 2):64 * (hd % 2) + 64], scalar1=ltb[0:n, n - 1:n], scalar2=None, op0=ALU.mult)
                        c2 = hd // 2
                        MM(ps[4][:, c2 * 128:c2 * 128 + n], xdtp[0:n, hd, :], mtb[0:n, 0:n], hd % 2 == 0, False, ["xdtp", "mtb"], [pk(4)])
                        MM(ps[4][:, c2 * 128:c2 * 128 + n], hbfp[:, hd, :], cst[:, 0:n], False, hd % 2 == 1, ["hbfp", "cst"], [pk(4)])
                        MM(ps[5][:, hd * 64:(hd + 1) * 64], btm[0:n, g2, :], xdd[0:n, :], True, True, ["btm", "xdd"], [pk(5)])
                        V('dve', 'scalar_tensor_tensor', ["hst", "wc", pk(5)], ["hst"], out=hst[:, hd, :], in0=hst[:, hd, :], scalar=erow[:, n - 1:n], in1=ps[5][:, hd * 64:(hd + 1) * 64], op0=ALU.mult, op1=ALU.add)
                        V('pool', 'tensor_copy', ["hst"], ["hbfp"], out=hbfp[:, hd, 64 * (hd % 2):64 * (hd % 2) + 64], in_=hst[:, hd, :])
                    for c2 in range(4):
                        V('dve', 'scalar_tensor_tensor', ["SL%d_%d" % (c2 // 2, g), "dcol", pk(4)], ["wb"], out=yv[:, c2, 0:n], in0=slot(c2)[:, ca:ca + n], scalar=dcol[:, c2:c2 + 1],
                          in1=ps[4][:, c2 * 128:c2 * 128 + n], op0=ALU.mult, op1=ALU.add)
                        ACT(szb[:, c2, 0:n], actT[:, c2, ca:ca + n], AF.Silu, ["act%d_%d" % (c2, g)], ["wa"])
                        V('pool', 'tensor_tensor', ["wb", "wa"], ["wb"], out=yv[:, c2, 0:n], in0=yv[:, c2, 0:n], in1=szb[:, c2, 0:n], op=ALU.mult)
                        ACT(sqb[:, c2, 0:n], yv[:, c2, 0:n], AF.Square, ["wb"], ["ba"])
                    for g2 in range(2):
                        for cc in range(2):
                            MM(ps[6][:, g2 * 128:g2 * 128 + n], onesb[:], sqb[:, 2 * g2 + cc, 0:n], cc == 0, cc == 1, ["ba", "onesb"], [pk(6)])
                    ACT(wk["wc"][:, 0:256], ps[6][:, 0:256], AF.Sqrt, [pk(6)], ["wc"], bias=EPS_AP[:, 0:1], scale=1.0 / 256.0)
                    V('dve', 'reciprocal', ["wc"], ["wd"], out=wk["wd"][:, 0:256], in_=wk["wc"][:, 0:256])
                    for c2 in range(4):
                        g2 = c2 // 2
                        V('dve', 'scalar_tensor_tensor', ["wb", "cw", "wd"], ["act%d_%d" % (c2, g)], out=actT[:, c2, ca:ca + n], in0=yv[:, c2, 0:n], scalar=nw[:, c2:c2 + 1],
                          in1=wk["wd"][:, g2 * 128:g2 * 128 + n], op0=ALU.mult, op1=ALU.mult)
                i = nxt("stg", 2)
                for q in range(4):
                    TR(ps[0][:, q * 128:(q + 1) * 128], hst[:, 2 * q:2 * q + 2, :].rearrange("p h q -> p (h q)"), identf[:], ["hst", "c_ident"], [pk(0)])
                V('dve', 'tensor_copy', [pk(0)], ["stg%d" % i], out=stg[i][:, 0:512], in_=ps[0][:, :])
                dst = dr["p_ssm"][l, pi] if kind == 'p' else dr["s_ssm"][l, si]
                DMA(dst.rearrange("(q p) n -> p q n", p=128), stg[i][:, 0:512].rearrange("p (q n) -> p q n", n=128), ["stg%d" % i], [], stream="stg%d" % i, eng='pool')

        def ffn(l, pi, si):
            if si is not None:
                load_T(dr["state_ffn_conv"][l, si, :, :], 2, 22, wk["wa"][:, 0:44].rearrange("p (c r) -> p c r", r=2), ["wa"])
            V('pool', 'memset', [], ["cvp"], ap=cvp[:, 0:3], constant=0.0)
            for jg in range(6):
                j0 = 4 * jg
                nj = min(4, 22 - j0)
                for jj in range(nj):
                    j = j0 + jj
                    wt, wkey = load_w(wsrc(dr["ffn_w_gate"][l], 0, 8, j * 128, 128))

                    def evg(g, pap, pkey):
                        a, n = GR[g]
                        if g == 0:
                            ACT(cvs[:, 3:19], pap[:, 0:16], AF.Copy, [pkey], ["cvs"])
                            ACT(cvp[:, 3:19], pap[:, 16:32], AF.Copy, [pkey], ["cvp"])
                        else:
                            ACT(cvp[:, 3 + a - 16:3 + a - 16 + n], pap, AF.Copy, [pkey], ["cvp"])
                    proj_fm(wt, wkey, 128, evg)
                    pt, pkey = proj_tm(wt, wkey, NT - 2, 2, 128)
                    i = nxt("kvo", 2)
                    V('dve', 'tensor_copy', [pkey], ["kvo%d" % i], out=kvo[i][0:2, :], in_=pt)
                    DMA(dr["p_ffn_conv"][l, pi, :, j * 128:(j + 1) * 128], kvo[i][0:2, :], ["kvo%d" % i], [], stream="kvo%d" % i, eng='pool')
                    if si is not None:
                        pt, pkey = proj_tm(wt, wkey, 14, 2, 128)
                        i = nxt("kvo", 2)
                        V('dve', 'tensor_copy', [pkey], ["kvo%d" % i], out=kvo[i][0:2, :], in_=pt)
                        DMA(dr["s_ffn_conv"][l, si, :, j * 128:(j + 1) * 128], kvo[i][0:2, :], ["kvo%d" % i], [], stream="kvo%d" % i, eng='pool')
                        V('dve', 'tensor_copy', ["wa"], ["cvs"], out=cvs[:, 1:3], in_=wk["wa"][:, 2 * j:2 * j + 2])
                    wt, wkey = load_w(wsrc(dr["ffn_w_up"][l], 0, 8, j * 128, 128))

                    def evu(g, pap, pkey):
                        a, n = GR[g]
                        ACT(SL[:, 3, a:a + n], pap, AF.Copy, [pkey], ["SL3_%d" % g])
                    proj_fm(wt, wkey, 128, evu)
                    srcs = [(cvp, LP, 16, "cvp")] + ([(cvs, SSEQ, 0, "cvs")] if si is not None else [])
                    for (cv_, ln, col0, ckey) in srcs:
                        pos = 0
                        while pos < ln:
                            m = min(512, ln - pos)
                            acc = wk["wb"][:, 0:m]
                            V('dve', 'tensor_scalar', [ckey, "fcw"], ["wb"], out=acc, in0=cv_[:, 3 + pos:3 + pos + m], scalar1=fcw[:, j, 2:3], scalar2=fcb[:, j:j + 1], op0=ALU.mult, op1=ALU.add)
                            for i in range(2):
                                V('dve', 'scalar_tensor_tensor', [ckey, "fcw", "wb"], ["wb"], out=acc, in0=cv_[:, 1 + i + pos:1 + i + pos + m], scalar=fcw[:, j, i:i + 1], in1=acc, op0=ALU.mult, op1=ALU.add)
                            ACT(wk["wc"][:, 0:m], acc, AF.Silu, ["wb"], ["wc"])
                            c0_ = col0 + pos
                            g0 = grp_of_col(c0_); g1 = grp_of_col(c0_ + m - 1)
                            V('pool', 'tensor_tensor', ["wc"] + ["SL3_%d" % g for g in range(g0, g1 + 1)], ["act%d_%d" % (jj, g) for g in range(g0, g1 + 1)],
                              out=actT[:, jj, c0_:c0_ + m], in0=wk["wc"][:, 0:m], in1=SL[:, 3, c0_:c0_ + m], op=ALU.mult)
                            pos += m
                for c in range(8):
                    wt, wkey = load_w(wsrc(dr["ffn_w_down"][l], j0 * 128, nj, c * 128, 128))
                    for g, (a, n) in enumerate(GR):
                        if si is None and g == 0:
                            a, n = 16, 16
                        b = nxt("pj", 2)
                        for m in range(nj):
                            MM(ps[b][:, 0:n], wt[:, m, :], actT[:, m, a:a + n], m == 0, m == nj - 1, [wkey, "act%d_%d" % (m, g)], [pk(b)])
                        V('dve', 'tensor_tensor', [pk(b), "xT%d_%d" % (c, g)], ["xT%d_%d" % (c, g)], out=xT[:, c, a:a + n], in0=ps[b][:, 0:n], in1=xT[:, c, a:a + n], op=ALU.add)

        def load_x(pi, si):
            items = [(dr["meta_tokens"][:, :], 16, 16)]
            if si is not None:
                items.append((dr["x_sample"][si, :, :], 16, 0))
            for i in range(16):
                items.append((dr["x_prompt"][pi, i * 128:(i + 1) * 128, :], 128, 32 + 128 * i))
            for src, n, col in items:
                g = grp_of_col(col)
                for half in range(2):
                    i = nxt("stg", 2)
                    DMA(stg[i][0:n, :], src[:, half * 512:(half + 1) * 512], [], ["stg%d" % i], stream="stg%d" % i)
                    b = nxt("pj", 2)
                    for c in range(4):
                        TR(ps[b][:, c * 128:c * 128 + n], stg[i][0:n, c * 128:(c + 1) * 128], identf[0:n, 0:n], ["stg%d" % i, "c_ident"], [pk(b)])
                    V('dve', 'tensor_copy', [pk(b)], ["xT%d_%d" % (half * 4 + c, g) for c in range(4)], out=xT[:, half * 4:half * 4 + 4, col:col + n],
                      in_=ps[b][:, :].rearrange("p (c m) -> p c m", m=128)[:, :, 0:n])
            if si is None:
                V('pool', 'memset', [], ["xT%d_0" % c for c in range(8)], ap=xT[:, :, 0:16], constant=0.0)

        def final_out(pi, si):
            for g, (a, n) in enumerate(GR):
                rinv = rms_stats(g, lambda c: xT[:, c, a:a + n], lambda c: ["xT%d_%d" % (c, g)], 8, float(D))
                pieces = [(0, 16, 's'), (16, 16, 'm')] if g == 0 else [(j * 128, 128, 'p') for j in range(4)]
                for c in range(8):
                    V('dve', 'scalar_tensor_tensor', ["xT%d_%d" % (c, g), "wd", "gfin"], ["SLy"] + ["SL%d_%d" % (r_, g_) for r_ in range(4) for g_ in range(5)], out=SL[:, c // 2, (c % 2) * 512:(c % 2) * 512 + n], in0=xT[:, c, a:a + n],
                      scalar=gfin[:, c:c + 1], in1=rinv, op0=ALU.mult, op1=ALU.mult)
                for (po, pn, kind) in pieces:
                    if kind == 'm' or (kind == 's' and si is None):
                        continue
                    if kind == 's':
                        dst = dr["y_sample"][si, :, :]
                    else:
                        t0 = a - 32 + po
                        dst = dr["y_prompt"][pi, t0:t0 + pn, :]
                    for half in range(2):
                        i = nxt("stg", 2)
                        b = nxt("pj", 2)
                        for c in range(4):
                            cc = half * 4 + c
                            TR(ps[b][0:pn, c * 128:(c + 1) * 128], SL[:, cc // 2, (cc % 2) * 512 + po:(cc % 2) * 512 + po + pn], identf[:], ["SLy", "c_ident"] + ["SL%d_%d" % (r_, g_) for r_ in range(4) for g_ in range(5)], [pk(b)])
                        V('dve', 'tensor_copy', [pk(b)], ["stg%d" % i], out=stg[i][0:pn, :], in_=ps[b][0:pn, :])
                        DMA(dst[:, half * 512:(half + 1) * 512], stg[i][0:pn, :], ["stg%d" % i], [], stream="stg%d" % i, eng='pool')

        fsc = sb("fsc", [128, 1])
        ALLK = ["SL%d_%d" % (r_, g_) for r_ in range(4) for g_ in range(5)] + ["VA", "VB0", "VB1", "SLy", "cvp", "cvs"] + ["act%d_%d" % (m_, g_) for m_ in range(4) for g_ in range(5)]

        def fence():
            V('pool', 'memset', ALLK, ALLK + ["fsc"], ap=fsc[:], constant=0.0)
        for (pi, si) in units:
            fence()
            load_x(pi, si)
            for l in range(n_layers):
                load_layer_params(l)
                rmsnorm_to_hT(gmix)
                fence()
                if "da" in phases:
                    for h in range(4):
                        attn_head_group(l, pi, si, True, h)
                    out_proj(l, 0)
                fence()
                if "ssd" in phases:
                    ssd(l, pi, si)
                    out_proj(l, 1)
                fence()
                if "sb" in phases:
                    V('pool', 'memset', [], ["VB0"], ap=VB0, constant=0.0)
                    V('pool', 'memset', [], ["VB1"], ap=VB1, constant=0.0)
                    for h in range(4):
                        attn_head_group(l, pi, si, False, h)
                    out_proj(l, 2)
                rmsnorm_to_hT(gffn)
                fence()
                if "ffn" in phases:
                    ffn(l, pi, si)
            fence()
            final_out(pi, si)
        P.finish()
        P.emit(nc, es)
    return nc, P


def kernel(**inputs):
    consts = host_consts()
    ins = {k: np.ascontiguousarray(np.asarray(v), dtype=np.float32) for k, v in inputs.items()}
    units = [(0, 0), (1, 1), (2, None), (3, None)]
    nc, P = build(units)
    in_maps = []
    for c in range(NCORES):
        m = {}
        for k, shp in IN_SHAPES.items():
            a = ins[k]
            if k == 'x_prompt':
                a = a[4 * c:4 * c + 4]
            elif k == 'x_sample':
                a = a[2 * c:2 * c + 2]
            elif k in ('cache_da_k', 'cache_da_v', 'cache_sb_k', 'cache_sb_v', 'state_ssm', 'state_ssm_conv', 'state_ffn_conv'):
                a = a[:, 2 * c:2 * c + 2]
            m[k] = np.ascontiguousarray(a.reshape(shp))
        m.update(consts)
        in_maps.append(m)
    res = run_bass_kernel_spmd(nc, in_maps, core_ids=list(range(NCORES)))
    full = dict(
        y_prompt=(NB, SEQ, D), y_sample=(NSB, SSEQ, D),
        p_da_k=(DEPTH, NB, LP, 4, 2, 64), p_da_v=(DEPTH, NB, LP, 4, 128), p_sb_k=(DEPTH, NB, LP, 8, 64), p_sb_v=(DEPTH, NB, LP, 8, 64),
        p_ssm=(DEPTH, NB, 8, 64, 128), p_ssm_conv=(DEPTH, NB, 3, 1024), p_ffn_conv=(DEPTH, NB, 2, DFF),
        s_da_k=(DEPTH, NSB, SSEQ, 4, 2, 64), s_da_v=(DEPTH, NSB, SSEQ, 4, 128), s_sb_k=(DEPTH, NSB, SSEQ, 8, 64), s_sb_v=(DEPTH, NSB, SSEQ, 8, 64),
        s_ssm=(DEPTH, NSB, 8, 64, 128), s_ssm_conv=(DEPTH, NSB, 3, 1024), s_ffn_conv=(DEPTH, NSB, 2, DFF))
    outs = []
    for k in OUT_ORDER:
        parts = [np.asarray(res.results[c][k]) for c in range(NCORES)]
        if k.startswith('y_'):
            a = np.concatenate(parts, axis=0)
        else:
            a = np.concatenate(parts, axis=1)
        outs.append(np.ascontiguousarray(a.reshape(full[k]).astype(np.float32)))
    return tuple(outs)
```

```python
import math
from contextlib import ExitStack
import numpy as np
import concourse.bass as bass
import concourse.mybir as mybir
from concourse.bass_utils import run_bass_kernel_spmd

F32 = mybir.dt.float32
BF16 = mybir.dt.bfloat16
AF = mybir.ActivationFunctionType
ALU = mybir.AluOpType
AX = mybir.AxisListType

D = 1024; DEPTH = 4; NB = 32; SEQ = 2048; NSB = 16; SSEQ = 16; PAST = 2048
NMETA = 16; LP = SEQ + NMETA
DIN = 4616; DMIX = 1536; DFF = 2816
OFF_DA_K = 512; OFF_DA_V = 1024; OFF_Z = 1536; OFF_XBC = 2048; OFF_DT = 3072; OFF_SB_Q = 3080; OFF_SB_K = 3592; OFF_SB_V = 4104
NT = 2080
EPS = 1e-6
NEG = -30000.0
NCORES = 8
SEM_LIMIT = 30000


class Prog:
    def __init__(self):
        self.ins = []
        self.last_w = {}
        self.readers = {}
        self.last_stream = {}

    def op(self, eng, fn, reads=(), writes=(), stream=None):
        i = len(self.ins)
        psr = [k for k in reads if k.startswith("ps")]
        if psr:
            reads = [k for k in reads if not k.startswith("ps")]
            writes = list(writes) + psr
        dom = ('dma', stream) if stream is not None else ('eng', eng)
        deps = set()
        for k in reads:
            w = self.last_w.get(k)
            if w is not None:
                deps.add(w)
        for k in writes:
            w = self.last_w.get(k)
            if w is not None:
                deps.add(w)
            deps.update(self.readers.get(k, {}).values())
        if stream is not None:
            p = self.last_stream.get(stream)
            if p is not None:
                deps.add(p)
            self.last_stream[stream] = i
        best = {}
        for d in deps:
            dd = self.ins[d]
            if dd['dom'] == dom and dom == ('eng', 'pe'):
                continue
            if dd['dom'] not in best or best[dd['dom']] < d:
                best[dd['dom']] = d
        for d in best.values():
            self.ins[d]['needs_inc'] = True
        self.ins.append(dict(eng=eng, fn=fn, deps=sorted(best.values()), dom=dom, needs_inc=False))
        for k in reads:
            self.readers.setdefault(k, {})[dom] = i
        for k in writes:
            self.last_w[k] = i
            self.readers[k] = {}
        return i

    def finish(self, eng='sp'):
        deps = list(self.last_stream.values())
        for d in deps:
            self.ins[d]['needs_inc'] = True
        self.ins.append(dict(eng=eng, fn=None, deps=sorted(deps), dom=('eng', eng), needs_inc=False))

    def emit(self, nc, es):
        sems = {}
        cnt = {}
        nsem = [0]

        def get_sem(dom, ep):
            key = (dom, ep)
            if key not in sems:
                sems[key] = es.enter_context(nc.semaphore("s%d" % nsem[0]))
                nsem[0] += 1
            return key

        known = {}
        plan = {e: [] for e in ('pe', 'act', 'dve', 'pool', 'sp')}
        for it in self.ins:
            waits = []
            kn = known.setdefault(it['eng'], {})
            for d in it['deps']:
                sk, val = self.ins[d]['sem']
                if kn.get(sk, 0) >= val:
                    continue
                kn[sk] = val
                waits.append((sk, val))
            inc = None
            if it['needs_inc']:
                dom = it['dom']
                step = 16 if dom[0] == 'dma' else 1
                ep, v = cnt.get(dom, (0, 0))
                if v + step > SEM_LIMIT:
                    ep, v = ep + 1, 0
                v += step
                cnt[dom] = (ep, v)
                sk = get_sem(dom, ep)
                it['sem'] = (sk, v)
                inc = (sk, step)
            plan[it['eng']].append((it['fn'], waits, inc))
        self.nsem = nsem[0]
        block = es.enter_context(nc.Block())

        def run(e, lst):
            for fn, waits, inc in lst:
                if fn is None:
                    for sk, val in waits:
                        e.wait_ge(sems[sk], val)
                    continue
                for sk, val in waits[:-1]:
                    e.wait_ge(sems[sk], val)
                r = fn(e)
                if waits:
                    r._wait_ge(sems[waits[-1][0]], waits[-1][1])
                if inc is not None:
                    r.then_inc(sems[inc[0]], inc[1])

        @block.tensor
        def _(e):
            run(e, plan['pe'])

        @block.scalar
        def _(e):
            run(e, plan['act'])

        @block.vector
        def _(e):
            run(e, plan['dve'])

        @block.gpsimd
        def _(e):
            run(e, plan['pool'])

        @block.sync
        def _(e):
            run(e, plan['sp'])


def rel_bucket_np(rel):
    half = 16; max_exact = 8
    n = np.abs(rel)
    nf = np.maximum(n, 1).astype(np.float32)
    large = max_exact + (np.log(nf / max_exact) / math.log(128 / max_exact) * (half - max_exact)).astype(np.int32)
    large = np.minimum(large, half - 1)
    return np.where(rel > 0, half, 0) + np.where(n < max_exact, n, large)


def host_consts():
    c = {}
    r = np.arange(-255, 257)
    b = rel_bucket_np(r)
    eh = np.zeros((32, 512), np.float32)
    eh[b, np.arange(512)] = 1.0
    c['c_eh'] = eh
    p = np.arange(128)
    c['c_ident'] = np.eye(128, dtype=np.float32)
    c['c_antiid'] = np.eye(128, dtype=np.float32)[:, ::-1].copy()
    kl = p[:, None]; ql = p[None, :]
    c['c_damask'] = np.where((kl // 64) > (ql // 64), NEG, 0.0).astype(np.float32)
    c['c_sbmask01'] = (kl < ql).astype(np.float32)
    c['c_sbmaskneg'] = np.where(kl < ql, 0.0, NEG).astype(np.float32)
    c['c_ssdmaskneg'] = np.where(ql >= kl, 0.0, NEG).astype(np.float32)
    c['c_negtri'] = np.where(kl >= ql, -1.0, 0.0).astype(np.float32)
    sel = np.zeros((16, 16 * 128), np.float32)
    for k in range(16):
        sel[k, k * 128:(k + 1) * 128] = 1.0
    c['c_sel'] = sel
    blk = np.zeros((128, 128), np.float32)
    blk[:64, :64] = 1.0; blk[64:, 64:] = 1.0
    c['c_ones_blk'] = blk
    return c


CONST_SHAPES = {k: v.shape for k, v in host_consts().items()}

IN_SHAPES = dict(
    x_prompt=(4, SEQ, D), x_sample=(2, SSEQ, D),
    cache_da_k=(DEPTH, 2, PAST, 512), cache_da_v=(DEPTH, 2, PAST, 512),
    cache_sb_k=(DEPTH, 2, PAST, 512), cache_sb_v=(DEPTH, 2, PAST, 512),
    state_ssm=(DEPTH, 2, 8, 64, 128), state_ssm_conv=(DEPTH, 2, 3, 1024), state_ffn_conv=(DEPTH, 2, 2, DFF),
    meta_tokens=(NMETA, D), rel_bias_table=(32, 4), w_in=(DEPTH, D, DIN), w_out=(DEPTH, DMIX, D),
    norm_mix=(DEPTH, D), norm_ffn=(DEPTH, D), da_lambda_q1=(DEPTH, 64), da_lambda_k1=(DEPTH, 64),
    da_lambda_q2=(DEPTH, 64), da_lambda_k2=(DEPTH, 64), da_subln=(DEPTH, 128),
    ssm_conv_w=(DEPTH, 4, 1024), ssm_conv_b=(DEPTH, 1024), ssm_dt_bias=(DEPTH, 8), ssm_a_log=(DEPTH, 8),
    ssm_d=(DEPTH, 8), ssm_norm=(DEPTH, 512), ffn_w_gate=(DEPTH, D, DFF), ffn_w_up=(DEPTH, D, DFF),
    ffn_w_down=(DEPTH, DFF, D), ffn_conv_w=(DEPTH, 3, DFF), ffn_conv_b=(DEPTH, DFF), final_norm=(D,),
)
OUT_SHAPES = dict(
    y_prompt=(4, SEQ, D), y_sample=(2, SSEQ, D),
    p_da_k=(DEPTH, 4, LP, 512), p_da_v=(DEPTH, 4, LP, 512), p_sb_k=(DEPTH, 4, LP, 512), p_sb_v=(DEPTH, 4, LP, 512),
    p_ssm=(DEPTH, 4, 512, 128), p_ssm_conv=(DEPTH, 4, 3, 1024), p_ffn_conv=(DEPTH, 4, 2, DFF),
    s_da_k=(DEPTH, 2, SSEQ, 512), s_da_v=(DEPTH, 2, SSEQ, 512), s_sb_k=(DEPTH, 2, SSEQ, 512), s_sb_v=(DEPTH, 2, SSEQ, 512),
    s_ssm=(DEPTH, 2, 512, 128), s_ssm_conv=(DEPTH, 2, 3, 1024), s_ffn_conv=(DEPTH, 2, 2, DFF),
)
OUT_ORDER = ['y_prompt', 'y_sample', 'p_da_k', 'p_da_v', 'p_sb_k', 'p_sb_v', 'p_ssm', 'p_ssm_conv', 'p_ffn_conv',
             's_da_k', 's_da_v', 's_sb_k', 's_sb_v', 's_ssm', 's_ssm_conv', 's_ffn_conv']


PHASES = ("da", "ssd", "sb", "ffn")


def build(units, n_layers=DEPTH, phases=PHASES, dbg=9):
    nc = bass.Bass("TRN2", target_bir_lowering=False)
    dr = {}
    for k, shp in IN_SHAPES.items():
        dr[k] = nc.dram_tensor(k, list(shp), F32, kind="ExternalInput").ap()
    for k, shp in CONST_SHAPES.items():
        dr[k] = nc.dram_tensor(k, list(shp), F32, kind="ExternalInput").ap()
    for k, shp in OUT_SHAPES.items():
        dr[k] = nc.dram_tensor(k, list(shp), F32, kind="ExternalOutput").ap()
    tvs = nc.dram_tensor("tvec_scratch", [4, 512], F32, kind="Internal").ap()

    P = Prog()
    es = ExitStack()
    with es:
        def sb(name, shape, dt=F32):
            return es.enter_context(nc.sbuf_tensor(name, shape, dt))

        xT = sb("xT", [128, 8, NT]); hT = sb("hT", [128, 8, NT], BF16); actT = sb("actT", [128, 4, NT], BF16)
        SL = sb("SL", [128, 4, NT])
        SLb = [SL[:, i, :].bitcast(BF16) for i in range(4)]

        def slot(i):
            return SLb[i // 2][:, (i % 2) * NT:(i % 2 + 1) * NT]
        VA, VB0, VB1 = [SLb[r][:, 0:17 * 128].rearrange("p (b n) -> p b n", n=128) for r in (1, 2, 3)]
        ksn = sb("ksn", [128, 16], BF16); vsn = sb("vsn", [16, 128], BF16)
        wst = [sb("wst%d" % i, [128, 8, 128]) for i in range(1)]
        wbf = [sb("wbf%d" % i, [128, 8, 128], BF16) for i in range(2)]
        stg = [sb("stg%d" % i, [128, 512]) for i in range(2)]
        kvo = [sb("kvo%d" % i, [128, 128]) for i in range(2)]
        cvp = sb("cvp", [128, NT + 4]); cvs = sb("cvs", [128, 3 + SSEQ])
        wk = {}
        for nm, shp, dt in [("wa", [128, 512], F32), ("wb", [128, 512], F32), ("wc", [128, 512], F32), ("wd", [128, 512], F32),
                            ("ba", [128, 512], BF16), ("bb", [128, 512], BF16), ("bc", [128, 512], BF16), ("bd", [128, 512], BF16)]:
            wk[nm] = sb(nm, shp, dt)
        identf = sb("identf", [128, 128]); identb = sb("identb", [128, 128], BF16); antiid = wk["wb"][:, 0:128]
        onesb = sb("onesb", [128, 128], BF16); negones = sb("negones", [128, 128], BF16); negtri = sb("negtri", [128, 128], BF16)
        onesblk = sb("onesblk", [128, 128], BF16)
        damask = wk["wb"][:, 128:256]; sbmask01 = sb("sbmask01", [128, 128]); sbmaskneg = sb("sbmaskneg", [128, 128])
        ssdmaskneg = sb("ssdmaskneg", [128, 128]); selT = sb("selT", [64, 8, 128])
        BT0 = sb("BT0", [128, 4, 128], BF16); BTm1 = sb("BTm1", [128, 4, 128], BF16); BTme = sb("BTme", [128, 4, 128], BF16); CH = sb("CH", [128, 4, 128], BF16); sbmasknegb = sb("sbmasknegb", [128, 128], BF16)
        chcol = sb("chcol", [128, 4]); tab = sb("tab", [32, 4])
        hank = wk["wd"][:, 256:384]
        gmix = sb("gmix", [128, 8]); gffn = sb("gffn", [128, 8]); gfin = sb("gfin", [128, 8])
        lamv = wk["wd"][:, 0:256].rearrange("p (a b) -> p a b", b=64); lamw = sb("lamw", [128, 8]); neglam = sb("neglam", [128, 1]); gsub = sb("gsub", [128, 1])
        cw = sb("cw", [128, 8, 4]); cb = sb("cb", [128, 8]); dtb = sb("dtb", [8, 4]); dcol = sb("dcol", [128, 4]); nw = sb("nw", [128, 4])
        fcw = sb("fcw", [128, 22, 3]); fcb = sb("fcb", [128, 22]); prm = sb("prm", [8, 512])
        dcT = cvp[0:64, 0:NT]; ones8 = sb("ones8", [8, 128]); dctm = sb("dctm", [128, 64]); negcs = sb("negcs", [128, 8])
        xdtp = sb("xdtp", [128, 8, 128], BF16); hbfp = sb("hbfp", [128, 8, 128], BF16); hst = sb("hst", [128, 8, 64])
        btm = sb("btm", [128, 2, 128], BF16); erow = wk["wc"][:, 256:384]; ltb = wk["wc"][:, 384:512]; mtb = sb("mtb", [128, 128], BF16)
        cst = sb("cst", [128, 128], BF16); xdd = sb("xdd", [128, 64], BF16); yv = wk["wb"][:, :].rearrange("p (c n) -> p c n", n=128); szb = wk["wa"][:, :].rearrange("p (c n) -> p c n", n=128)
        sqb = wk["ba"][:, :].rearrange("p (c n) -> p c n", n=128);
        ps = [es.enter_context(nc.psum_tensor("ps%d" % i, [128, 512], F32)) for i in range(8)]

        def pk(b):
            return "ps%d" % b

        def MM(out, lhsT, rhs, start, stop, r, w):
            P.op('pe', lambda e: e.matmul(out, lhsT=lhsT, rhs=rhs, start=start, stop=stop), r, w)

        def TR(out, in_, ident, r, w):
            P.op('pe', lambda e: e.transpose(out, in_, ident), r, w)

        def ACT(out, in_, func, r, w, bias=None, scale=None, accum_out=None):
            kw = {}
            if bias is not None: kw['bias'] = bias
            if scale is not None: kw['scale'] = scale
            if accum_out is not None: kw['accum_out'] = accum_out
            P.op('act', lambda e: e.activation(out=out, in_=in_, func=func, **kw), r, w)

        def V(eng, name, r, w, **kw):
            P.op(eng, lambda e: getattr(e, name)(**kw), r, w)

        def DMA(out, in_, r, w, stream, eng='sp', slow=False):
            if slow:
                P.op(eng, lambda e: e.dma_start(out=out, in_=in_, allow_slow_non_contiguous=True), r, w, stream=stream)
            else:
                P.op(eng, lambda e: e.dma_start(out=out, in_=in_), r, w, stream=stream)

        rot = {}

        def nxt(name, n):
            v = rot.get(name, 0)
            rot[name] = (v + 1) % n
            return v

        GR = [(0, 32)] + [(32 + 512 * i, 512) for i in range(4)]

        def grp_of_col(c):
            return 0 if c < 32 else (c - 32) // 512 + 1

        cload = [("c_ident", identf), ("c_sbmask01", sbmask01),
                 ("c_sbmaskneg", sbmaskneg), ("c_ssdmaskneg", ssdmaskneg)]
        for nm, t in cload:
            DMA(t[:], dr[nm][:, :], [], [nm], stream="c_" + nm)
        DMA(antiid, dr["c_antiid"][:, :], [], ["wb"], stream="c_wb")
        DMA(damask, dr["c_damask"][:, :], ["wb"], ["wb"], stream="c_wb")
        DMA(tab[:], dr["rel_bias_table"][:, :], [], ["tab"], stream="c_tab")
        DMA(wk["wa"][:, 0:128], dr["c_negtri"][:, :], [], ["wa"], stream="c_wa")
        V('dve', 'tensor_copy', ["wa"], ["negtri"], out=negtri[:], in_=wk["wa"][:, 0:128])
        DMA(wk["wa"][:, 128:256], dr["c_ones_blk"][:, :], ["wa"], ["wa"], stream="c_wa")
        V('dve', 'tensor_copy', ["wa"], ["onesblk"], out=onesblk[:], in_=wk["wa"][:, 128:256])
        V('dve', 'tensor_copy', ["c_ident"], ["identb"], out=identb[:], in_=identf[:])
        V('dve', 'tensor_copy', ["c_sbmaskneg"], ["sbmasknegb"], out=sbmasknegb[:], in_=sbmaskneg[:])
        V('pool', 'memset', [], ["onesb"], ap=onesb[:], constant=1.0)
        V('pool', 'memset', [], ["negones"], ap=negones[:], constant=-1.0)
        V('pool', 'memset', [], ["ones8"], ap=ones8[:], constant=1.0)
        V('pool', 'memset', [], ["cvp"], ap=cvp[:], constant=0.0)
        V('pool', 'memset', [], ["cvs"], ap=cvs[:], constant=0.0)
        V('pool', 'memset', [], ["xdtp"], ap=xdtp[:], constant=0.0)
        V('pool', 'memset', [], ["hbfp"], ap=hbfp[:], constant=0.0)
        V('pool', 'memset', [], ["xTall"] + ["xT%d_%d" % (c, g) for c in range(8) for g in range(5)], ap=xT[:], constant=0.0)
        for i in range(4):
            V('pool', 'memset', [], ["SL%d_%d" % (i, g) for g in range(5)], ap=SL[:, i, :], constant=0.0)
        V('pool', 'memset', [], ["selT"], ap=selT[:], constant=0.0)
        DMA(selT[32:40, :, :], dr["c_sel"][0:8, 0:1024].rearrange("k (h m) -> k h m", h=8), ["selT"], ["selT"], stream="c_sel")
        DMA(wk["wc"][0:32, :], dr["c_eh"][:, :], [], ["wc"], stream="c_eh")
        MM(ps[0][0:4, :], tab[:], wk["wc"][0:32, :], True, True, ["tab", "wc"], [pk(0)])
        V('dve', 'tensor_copy', [pk(0)], ["wd"], out=wk["wd"][0:4, :], in_=ps[0][0:4, :])
        DMA(tvs[:, :], wk["wd"][0:4, :], ["wd"], ["tvs"], stream="c_tvs")
        from concourse.ap import AP as _AP
        for h in range(4):
            for ti, (dst, off, msk) in enumerate([(BT0, 128, True), (BTm1, 0, False), (BTme, 112, False)]):
                src = _AP(tvs.tensor, tvs[h, off:off + 1].offset, [[1, 128], [1, 128]])
                DMA(hank, src, ["tvs"], ["wd"], stream="c_hank")
                MM(ps[1][:, 0:128], hank, antiid, True, True, ["wd", "wb"], [pk(1)])
                if msk:
                    V('dve', 'scalar_tensor_tensor', [pk(1), "wb"], ["BT"], out=dst[:, h, :], in0=ps[1][:, 0:128], scalar=8.0, in1=damask, op0=ALU.mult, op1=ALU.add)
                else:
                    V('dve', 'tensor_scalar', [pk(1)], ["BT"], out=dst[:, h, :], in0=ps[1][:, 0:128], scalar1=8.0, scalar2=None, op0=ALU.mult)
            src = _AP(tvs.tensor, tvs[h, 0:1].offset, [[0, 128], [1, 1]])
            DMA(chcol[:, h:h + 1], src, ["tvs"], ["chcol"], stream="c_chc")
            ACT(CH[:, h, :], identf[:], AF.Identity, ["chcol", "c_ident"], ["BT"], bias=chcol[:, h:h + 1], scale=0.0)
            V('dve', 'tensor_scalar', ["BT"], ["BT"], out=CH[:, h, :], in0=CH[:, h, :], scalar1=8.0, scalar2=None, op0=ALU.mult)

        def load_T(src2d, r, nch, dst3, keys_w, width=128):
            c = 0
            while c < nch:
                m = min(nch - c, 4)
                DMA(prm[0:r, 0:m * width], src2d[:, c * width:(c + m) * width], ["prm"], ["prm"], stream="prm")
                for j in range(m):
                    TR(ps[0][0:width, j * r:(j + 1) * r], prm[0:r, j * width:(j + 1) * width], identf[0:r, 0:r], ["prm", "c_ident"], [pk(0)])
                V('dve', 'tensor_copy', [pk(0)], keys_w, out=dst3[:, c:c + m, :], in_=ps[0][0:width, 0:m * r].rearrange("p (j r) -> p j r", r=r))
                c += m
        load_T(dr["final_norm"].rearrange("(o n) -> o n", o=1), 1, 8, gfin[:].rearrange("p (c o) -> p c o", o=1), ["gfin"])

        def load_w(src3):
            nch, ncols = src3.shape[1], src3.shape[2]
            i = nxt("wst", 1); j = nxt("wbf", 2)
            DMA(wst[i][:, 0:nch, 0:ncols], src3, [], ["wst%d" % i], stream="wst%d" % i)
            V('pool', 'tensor_copy', ["wst%d" % i], ["wbf%d" % j], out=wbf[j][:, 0:nch, 0:ncols], in_=wst[i][:, 0:nch, 0:ncols])
            return wbf[j][:, 0:nch, 0:ncols], "wbf%d" % j

        def wsrc(w2d, r0, nch, c0, ncols):
            return w2d[r0:r0 + nch * 128, c0:c0 + ncols].rearrange("(kc p) n -> p kc n", p=128)

        def hkeys(g):
            return ["hT_%d" % g]

        def proj_fm(wt, wkey, ncols, evac):
            for g, (a, n) in enumerate(GR):
                b = nxt("pj", 2)
                for kc in range(8):
                    MM(ps[b][0:ncols, 0:n], wt[:, kc, :], hT[:, kc, a:a + n], kc == 0, kc == 7, [wkey] + hkeys(g), [pk(b)])
                evac(g, ps[b][0:ncols, 0:n], pk(b))

        def proj_tm(wt, wkey, a, n, ncols):
            b = nxt("pj", 2)
            g = grp_of_col(a)
            for kc in range(8):
                MM(ps[b][0:n, 0:ncols], hT[:, kc, a:a + n], wt[:, kc, :], kc == 0, kc == 7, [wkey] + hkeys(g), [pk(b)])
            return ps[b][0:n, 0:ncols], pk(b)

        def rms_stats(g, src_ap_fn, src_keys, nchunks, denom):
            a, n = GR[g] if isinstance(g, int) else g
            b = nxt("pj", 2)
            for c in range(nchunks):
                t = wk["ba"] if c % 2 == 0 else wk["bb"]
                tk = "ba" if c % 2 == 0 else "bb"
                ACT(t[:, 0:n], src_ap_fn(c), AF.Square, src_keys(c), [tk])
                MM(ps[b][:, 0:n], onesb[:], t[:, 0:n], c == 0, c == nchunks - 1, [tk, "onesb"], [pk(b)])
            ACT(wk["wc"][:, 0:n], ps[b][:, 0:n], AF.Sqrt, [pk(b)], ["wc"], bias=EPS_AP[:, 0:1], scale=1.0 / denom)
            V('dve', 'reciprocal', ["wc"], ["wd"], out=wk["wd"][:, 0:n], in_=wk["wc"][:, 0:n])
            return wk["wd"][:, 0:n]

        EPS_AP = sb("epsap", [128, 1])
        V('pool', 'memset', [], ["epsap"], ap=EPS_AP[:], constant=EPS)

        def rmsnorm_to_hT(gw):
            for g, (a, n) in enumerate(GR):
                rinv = rms_stats(g, lambda c: xT[:, c, a:a + n], lambda c: ["xT%d_%d" % (c, g)], 8, float(D))
                for c in range(8):
                    V('dve', 'scalar_tensor_tensor', ["xT%d_%d" % (c, g), "wd", "gw"], hkeys(g), out=hT[:, c, a:a + n], in0=xT[:, c, a:a + n],
                      scalar=gw[:, c:c + 1], in1=rinv, op0=ALU.mult, op1=ALU.mult)

        def p_blocks():
            return [(0, 16, 16)] + [(16 + 128 * i, 128, 32 + 128 * i) for i in range(16)]

        def s_blocks():
            return [(128 * i, 128, None) for i in range(16)] + [(2048, 16, 0)]

        P_QT = [[0], [1, 2, 3, 4], [5, 6, 7, 8], [9, 10, 11, 12], [13, 14, 15, 16]]
        S_QT = [[16]]

        def kgrp(kpos):
            return grp_of_col(kpos + 16)

        st = dict(layer=0, pi=0, si=None)

        def kv_out(wt, wkey, ncols, blocks_cols, out_fn, bf_dst_fn):
            for bi, a, n in blocks_cols:
                pt, pkey = proj_tm(wt, wkey, a, n, ncols)
                i = nxt("kvo", 2)
                V('dve', 'tensor_copy', [pkey], ["kvo%d" % i], out=kvo[i][0:n, 0:ncols], in_=pt)
                DMA(out_fn(bi), kvo[i][0:n, 0:ncols], ["kvo%d" % i], [], stream="kvo%d" % i, eng='pool')
                if bf_dst_fn is not None:
                    for dst, src, keys in bf_dst_fn(bi, pt):
                        ACT(dst, src, AF.Copy, [pkey], keys)

        def tok_blocks(has_s):
            l = [("p", bi, col, n) for bi, (kp, n, col) in enumerate(p_blocks())]
            if has_s:
                l.append(("s", 16, 0, 16))
            return l

        def load_cache(ck, cv, l, s, c0, ktslot, vdsts):
            for q in range(4):
                i = nxt("stg", 2)
                src = dr[ck][l, s, q * 512:(q + 1) * 512, c0:c0 + 128].rearrange("(b p) n -> p b n", p=128)
                DMA(stg[i][:, 0:512].rearrange("p (b n) -> p b n", n=128), src, [], ["stg%d" % i], stream="stg%d" % i)
                b = nxt("pj", 2)
                for j in range(4):
                    TR(ps[b][:, j * 128:(j + 1) * 128], stg[i][:, j * 128:(j + 1) * 128], identf[:], ["stg%d" % i, "c_ident"], [pk(b)])
                kp = q * 512
                V('dve', 'tensor_copy', [pk(b)], ["SL%d_%d" % (ktslot // 2, kgrp(kp)), "SL%d_%d" % (ktslot // 2, kgrp(kp + 511))],
                  out=slot(ktslot)[:, kp:kp + 512], in_=ps[b][:, :])
                i = nxt("stg", 2)
                src = dr[cv][l, s, q * 512:(q + 1) * 512, c0:c0 + 128].rearrange("(b p) n -> p b n", p=128)
                DMA(stg[i][:, 0:512].rearrange("p (b n) -> p b n", n=128), src, [], ["stg%d" % i], stream="stg%d" % i)
                for (vd, vkey, ca, cn) in vdsts:
                    V('pool', 'tensor_copy', ["stg%d" % i], [vkey], out=vd[:, 4 * q:4 * q + 4, ca:ca + cn],
                      in_=stg[i][:, 0:512].rearrange("p (b n) -> p b n", n=128)[:, :, ca:ca + cn])

        def load_layer_params(l):
            load_T(dr["norm_mix"][l:l + 1, :], 1, 8, gmix[:].rearrange("p (c o) -> p c o", o=1), ["gw"])
            load_T(dr["norm_ffn"][l:l + 1, :], 1, 8, gffn[:].rearrange("p (c o) -> p c o", o=1), ["gw2"])
            load_T(dr["ssm_conv_w"][l, :, :], 4, 8, cw[:], ["cw"])
            load_T(dr["ssm_conv_b"][l:l + 1, :], 1, 8, cb[:].rearrange("p (c o) -> p c o", o=1), ["cw"])
            load_T(dr["ssm_norm"][l:l + 1, :], 1, 4, nw[:].rearrange("p (c o) -> p c o", o=1), ["cw"])
            load_T(dr["ffn_conv_w"][l, :, :], 3, 22, fcw[:], ["fcw"])
            load_T(dr["ffn_conv_b"][l:l + 1, :], 1, 22, fcb[:].rearrange("p (c o) -> p c o", o=1), ["fcw"])
            lam_init = 0.8 - 0.6 * math.exp(-0.3 * l)
            load_T(dr["da_subln"][l:l + 1, :], 1, 1, lamw[:, 0:1].rearrange("p (c o) -> p c o", o=1), ["lamw"])
            V('dve', 'tensor_scalar', ["lamw"], ["gsub"], out=gsub[:], in0=lamw[:, 0:1], scalar1=(1.0 - lam_init), scalar2=None, op0=ALU.mult)
            for i, nm in enumerate(["da_lambda_q1", "da_lambda_k1", "da_lambda_q2", "da_lambda_k2"]):
                DMA(lamv[:, i, :], dr[nm][l:l + 1, :].partition_broadcast(128), [], ["wd"], stream="wd")
            for j in range(2):
                V('dve', 'tensor_tensor', ["wd"], ["lamt"], out=wk["wa"][:, 0:64], in0=lamv[:, 2 * j, :], in1=lamv[:, 2 * j + 1, :], op=ALU.mult)
                V('dve', 'reduce_sum', ["lamt"], ["lamw"], out=lamw[:, 2 + j:3 + j], in_=wk["wa"][:, 0:64], axis=AX.X)
            ACT(lamw[:, 4:6], lamw[:, 2:4], AF.Exp, ["lamw"], ["lamw"])
            V('dve', 'tensor_tensor', ["lamw"], ["lamw"], out=lamw[:, 6:7], in0=lamw[:, 5:6], in1=lamw[:, 4:5], op=ALU.subtract)
            V('dve', 'tensor_scalar', ["lamw"], ["neglam"], out=neglam[:], in0=lamw[:, 6:7], scalar1=-lam_init, scalar2=None, op0=ALU.add)
            DMA(dtb[:, 0:1], dr["ssm_dt_bias"][l:l + 1, :].rearrange("o h -> h o"), [], ["dtb"], stream="dtb", slow=True)
            DMA(dtb[:, 1:2], dr["ssm_a_log"][l:l + 1, :].rearrange("o h -> h o"), ["dtb"], ["dtb"], stream="dtb", slow=True)
            ACT(dtb[:, 2:3], dtb[:, 1:2], AF.Exp, ["dtb"], ["dtb"])
            V('dve', 'tensor_scalar', ["dtb"], ["dtb"], out=dtb[:, 3:4], in0=dtb[:, 2:3], scalar1=-1.0, scalar2=None, op0=ALU.mult)
            for c in range(4):
                for hh in range(2):
                    DMA(dcol[64 * hh:64 * hh + 64, c:c + 1], dr["ssm_d"][l:l + 1, 2 * c + hh:2 * c + hh + 1].partition_broadcast(64), [], ["dcol"], stream="dcol")

        def da_bias(kind, kb, qb, h):
            if kind == 'p':
                if qb == kb:
                    n = 16 if kb == 0 else 128
                    return BT0[0:n, h, 0:n]
                if qb == kb + 1:
                    return BTme[0:16, h, :] if kb == 0 else BTm1[:, h, :]
                return None
            else:
                if kb == 16:
                    return BT0[0:16, h, 0:16]
                if kb == 15:
                    return BTm1[:, h, 0:16]
                return None

        def attn_da(kind, blocks, qtiles, h, mixc):
            KT = slot(1); QT = slot(0)
            for qt in qtiles:
                qcols = [blocks[b][2] for b in qt]; qa = qcols[0]
                n = sum(blocks[b][1] for b in qt)
                qg = grp_of_col(qa)
                offs = {}
                o = 0
                for b in qt:
                    offs[b] = o; o += blocks[b][1]
                kmax = qt[-1]
                for kb in range(kmax + 1):
                    kp, ksz, _ = blocks[kb]
                    vq = [b for b in qt if b >= kb]
                    off = offs[vq[0]]
                    band = any(da_bias(kind, kb, b, h) is not None for b in vq)
                    for m in range(2):
                        zb = 2 + nxt("daz", 2)
                        kkeys = ["SL0_%d" % kgrp(kp), "SL0_%d" % qg]
                        MM(ps[zb][0:ksz, off:n], KT[64 * m:64 * m + 64, kp:kp + ksz], QT[64 * m:64 * m + 64, qa + off:qa + n], True, not band, kkeys, [pk(zb)])
                        if band:
                            for bi, b in enumerate(vq):
                                t = da_bias(kind, kb, b, h)
                                qsz = blocks[b][1]
                                if t is None:
                                    t = CH[0:ksz, h, 0:qsz]
                                MM(ps[zb][0:ksz, offs[b]:offs[b] + qsz], identb[0:ksz, 0:ksz], t, False, bi == len(vq) - 1, ["identb", "BT"], [pk(zb)])
                        ei = nxt("dae", 2)
                        et = wk["ba"] if ei == 0 else wk["bb"]; ek = "ba" if ei == 0 else "bb"
                        if band:
                            ACT(et[0:ksz, off:n], ps[zb][0:ksz, off:n], AF.Exp, [pk(zb)], [ek], scale=0.125)
                        else:
                            ACT(et[0:ksz, off:n], ps[zb][0:ksz, off:n], AF.Exp, [pk(zb), "chcol"], [ek], scale=0.125, bias=chcol[0:ksz, h:h + 1])
                        MM(ps[4 + m][:, off:n], VA[0:ksz, kb, :], et[0:ksz, off:n], kb == 0, kb == kmax, [ek, "VA"], [pk(4 + m)])
                        MM(ps[6 + m][:, off:n], onesb[0:ksz, :], et[0:ksz, off:n], kb == 0, kb == kmax, [ek, "onesb"], [pk(6 + m)])
                V('dve', 'reciprocal', [pk(6)], ["wa"], out=wk["wa"][:, 0:n], in_=ps[6][:, 0:n])
                V('dve', 'reciprocal', [pk(7)], ["wb"], out=wk["wb"][:, 0:n], in_=ps[7][:, 0:n])
                V('dve', 'tensor_tensor', [pk(4), "wa"], ["wa"], out=wk["wa"][:, 0:n], in0=ps[4][:, 0:n], in1=wk["wa"][:, 0:n], op=ALU.mult)
                V('dve', 'tensor_tensor', [pk(5), "wb"], ["wb"], out=wk["wb"][:, 0:n], in0=ps[5][:, 0:n], in1=wk["wb"][:, 0:n], op=ALU.mult)
                V('dve', 'scalar_tensor_tensor', ["wa", "wb", "neglam"], ["wa"], out=wk["wa"][:, 0:n], in0=wk["wb"][:, 0:n], scalar=neglam[:, 0:1],
                  in1=wk["wa"][:, 0:n], op0=ALU.mult, op1=ALU.add)
                rinv = rms_stats((qa, n), lambda c: wk["wa"][:, 0:n], lambda c: ["wa"], 1, 128.0)
                V('dve', 'scalar_tensor_tensor', ["wa", "wd", "gsub"], ["act%d_%d" % (mixc, qg)], out=actT[:, mixc, qa:qa + n], in0=wk["wa"][:, 0:n],
                  scalar=gsub[:, 0:1], in1=rinv, op0=ALU.mult, op1=ALU.mult)

        def kt_evac_split(dst_slot_idx, scale=None):
            KTd = slot(dst_slot_idx)
            si = dst_slot_idx // 2

            def f(g, pap, pkey):
                a, n = GR[g]
                if g == 0:
                    V('dve', 'tensor_copy', [pkey], ["ksn"], out=ksn[:, 0:16], in_=pap[:, 0:16])
                    V('dve', 'tensor_copy', [pkey], ["SL%d_0" % si], out=KTd[:, 0:16], in_=pap[:, 16:32])
                else:
                    V('dve', 'tensor_copy', [pkey], ["SL%d_%d" % (si, g)], out=KTd[:, a - 16:a - 16 + n], in_=pap)
            return f

        def qt_evac(scale):
            def f(g, pap, pkey):
                a, n = GR[g]
                if scale is None:
                    V('dve', 'tensor_copy', [pkey], ["SL0_%d" % g], out=slot(0)[:, a:a + n], in_=pap)
                else:
                    V('dve', 'tensor_scalar', [pkey], ["SL0_%d" % g], out=slot(0)[:, a:a + n], in0=pap, scalar1=scale, scalar2=None, op0=ALU.mult)
            return f

        def attn_head_group(l, pi, si, kind_da, hidx):
            if kind_da:
                qc, kc_, vc = hidx * 128, OFF_DA_K + hidx * 128, OFF_DA_V + hidx * 128
                okp, ovp, oks, ovs, ck, cv = "p_da_k", "p_da_v", "s_da_k", "s_da_v", "cache_da_k", "cache_da_v"
            else:
                qc, kc_, vc = OFF_SB_Q + hidx * 128, OFF_SB_K + hidx * 128, OFF_SB_V + hidx * 128
                okp, ovp, oks, ovs, ck, cv = "p_sb_k", "p_sb_v", "s_sb_k", "s_sb_v", "cache_sb_k", "cache_sb_v"
            c0 = hidx * 128
            w2 = dr["w_in"][l]
            wt, wkey = load_w(wsrc(w2, 0, 8, qc, 128))
            proj_fm(wt, wkey, 128, qt_evac(None if kind_da else 0.125))
            wt, wkey = load_w(wsrc(w2, 0, 8, kc_, 128))
            proj_fm(wt, wkey, 128, kt_evac_split(1))
            tb = tok_blocks(si is not None)
            pb = p_blocks()

            def kout(item):
                kind, bi, col, n = item
                if kind == 'p':
                    return dr[okp][l, pi, pb[bi][0]:pb[bi][0] + n, c0:c0 + 128]
                return dr[oks][l, si, :, c0:c0 + 128]

            def vout(item):
                kind, bi, col, n = item
                if kind == 'p':
                    return dr[ovp][l, pi, pb[bi][0]:pb[bi][0] + n, c0:c0 + 128]
                return dr[ovs][l, si, :, c0:c0 + 128]
            kv_out(wt, wkey, 128, [(it, it[2], it[3]) for it in tb], kout, None)
            wt, wkey = load_w(wsrc(w2, 0, 8, vc, 128))

            def vbf(item, pt):
                kind, bi, col, n = item
                if kind == 's':
                    return [(vsn[0:16, :], pt, ["vsn"])]
                if kind_da:
                    return [(VA[0:n, bi, :], pt, ["VA"])]
                return [(VB0[0:n, bi, 0:64], pt[:, 0:64], ["VB0"]), (VB1[0:n, bi, 64:128], pt[:, 64:128], ["VB1"])]
            kv_out(wt, wkey, 128, [(it, it[2], it[3]) for it in tb], vout, vbf)
            mixc = hidx
            if dbg >= 2:
                if kind_da:
                    attn_da('p', pb, P_QT, hidx, mixc)
                else:
                    attn_sb(pb, P_QT, mixc)
            if si is not None and dbg >= 3:
                vd = [(VA, "VA", 0, 128)] if kind_da else [(VB0, "VB0", 0, 64), (VB1, "VB1", 64, 64)]
                load_cache(ck, cv, l, si, c0, 1, vd)
                V('dve', 'tensor_copy', ["ksn"], ["SL0_4"], out=slot(1)[:, 2048:2064], in_=ksn[:, 0:16])
                for (vdst, vkey, ca, cn) in vd:
                    V('pool', 'tensor_copy', ["vsn"], [vkey], out=vdst[0:16, 16, ca:ca + cn], in_=vsn[0:16, ca:ca + cn])
                if kind_da:
                    attn_da('s', s_blocks(), S_QT, hidx, mixc)
                else:
                    attn_sb(s_blocks(), S_QT, mixc)

        def attn_sb(blocks, qtiles, mixc):
            KT = slot(1); QT = slot(0)
            for qt in qtiles:
                qa = blocks[qt[0]][2]
                n = sum(blocks[b][1] for b in qt)
                qg = grp_of_col(qa)
                offs = {}
                o = 0
                for b in qt:
                    offs[b] = o; o += blocks[b][1]
                kmax = qt[-1]
                V('dve', 'memset', [], [pk(7)], ap=ps[7][:, 0:n], constant=0.0)
                for hh in range(2):
                    Vp = VB0 if hh == 0 else VB1; vkey = "VB0" if hh == 0 else "VB1"
                    pr = slice(64 * hh, 64 * hh + 64)
                    for kb in range(kmax, -1, -1):
                        kp, ksz, _ = blocks[kb]
                        vq = [b for b in qt if b >= kb]
                        off = offs[vq[0]]
                        diag = (vq[0] == kb)
                        dsz = blocks[kb][1] if diag else 0
                        off2 = off + dsz
                        kkeys = ["SL0_%d" % kgrp(kp), "SL0_%d" % qg]
                        zb = 2 + nxt("sbz", 2)
                        MM(ps[zb][0:ksz, off:n], KT[pr, kp:kp + ksz], QT[pr, qa + off:qa + n], True, True, kkeys, [pk(zb)])
                        ACT(wk["wa"][0:ksz, off:n], ps[zb][0:ksz, off:n], AF.Exp, [pk(zb)], ["wa"])
                        si_ = nxt("sbsp", 2)
                        sp = wk["ba"] if si_ == 0 else wk["bb"]; spk = "ba" if si_ == 0 else "bb"
                        ACT(sp[0:ksz, off:n], wk["wa"][0:ksz, off:n], AF.Ln, ["wa"], [spk], bias=1.0)
                        if diag:
                            V('pool', 'tensor_tensor', [spk, "c_sbmask01"], [spk], out=sp[0:ksz, off:off2], in0=sp[0:ksz, off:off2], in1=sbmask01[0:ksz, 0:dsz], op=ALU.mult)
                        tb_ = 4 + nxt("sbt", 2)
                        MM(ps[tb_][0:ksz, off:n], KT[pr, kp:kp + ksz], QT[pr, qa + off:qa + n], True, False, kkeys, [pk(tb_)])
                        MM(ps[tb_][0:ksz, off:n], negtri[0:ksz, 0:ksz], sp[0:ksz, off:n], False, not diag, [spk, "negtri"], [pk(tb_)])
                        if diag:
                            MM(ps[tb_][0:ksz, off:off2], identb[0:ksz, 0:ksz], sbmasknegb[0:ksz, 0:dsz], False, True, ["identb", "sbmasknegb"], [pk(tb_)])
                        wi = nxt("sbw", 2)
                        wt_ = wk["bc"] if wi == 0 else wk["bd"]; wtk = "bc" if wi == 0 else "bd"
                        if diag:
                            ACT(wt_[0:ksz, off:off2], ps[tb_][0:ksz, off:off2], AF.Exp, [pk(tb_)], [wtk])
                        if off2 < n:
                            V('dve', 'tensor_tensor', [pk(tb_), "wc"], ["wb"], out=wk["wb"][0:ksz, off2:n], in0=ps[tb_][0:ksz, off2:n], in1=wk["wc"][0:ksz, off2:n], op=ALU.add)
                            ACT(wt_[0:ksz, off2:n], wk["wb"][0:ksz, off2:n], AF.Exp, ["wb"], [wtk])
                        if kb > 0:
                            MM(ps[6][:, off:n], negones[0:ksz, :], sp[0:ksz, off:n], True, True, [spk, "negones"], [pk(6)])
                            if diag:
                                V('dve', 'tensor_copy', [pk(6)], ["wc"], out=wk["wc"][:, off:off2], in_=ps[6][:, off:off2])
                            if off2 < n:
                                V('dve', 'tensor_tensor', [pk(6), "wc"], ["wc"], out=wk["wc"][:, off2:n], in0=ps[6][:, off2:n], in1=wk["wc"][:, off2:n], op=ALU.add)
                        first_h = (hh == 0)
                        if diag:
                            MM(ps[7][:, off:off2], Vp[0:ksz, kb, :], wt_[0:ksz, off:off2], False, False, [wtk, vkey], [pk(7)])
                        if off2 < n:
                            MM(ps[7][:, off2:n], Vp[0:ksz, kb, :], wt_[0:ksz, off2:n], False, (hh == 1 and kb == 0), [wtk, vkey], [pk(7)])
                ACT(actT[:, mixc, qa:qa + n], ps[7][:, 0:n], AF.Copy, [pk(7)], ["act%d_%d" % (mixc, qg)])

        def out_proj(l, mi):
            for c in range(8):
                wt, wkey = load_w(wsrc(dr["w_out"][l], mi * 512, 4, c * 128, 128))
                for g, (a, n) in enumerate(GR):
                    b = nxt("pj", 2)
                    for m in range(4):
                        MM(ps[b][:, 0:n], wt[:, m, :], actT[:, m, a:a + n], m == 0, m == 3, [wkey, "act%d_%d" % (m, g)], [pk(b)])
                    V('dve', 'tensor_tensor', [pk(b), "xT%d_%d" % (c, g)], ["xT%d_%d" % (c, g)], out=xT[:, c, a:a + n], in0=ps[b][:, 0:n], in1=xT[:, c, a:a + n], op=ALU.add)

        def ssd(l, pi, si):
            w2 = dr["w_in"][l]
            for c in range(4):
                wt, wkey = load_w(wsrc(w2, 0, 8, OFF_Z + c * 128, 128))

                def ev(g, pap, pkey, c=c):
                    a, n = GR[g]
                    V('dve', 'tensor_copy', [pkey], ["act%d_%d" % (c, g)], out=actT[:, c, a:a + n], in_=pap)
                proj_fm(wt, wkey, 128, ev)
            V('pool', 'memset', [], ["cvp"], ap=cvp[:, 0:3], constant=0.0)
            if si is not None:
                load_T(dr["state_ssm_conv"][l, si, :, :], 3, 8, wk["wa"][:, 0:24].rearrange("p (c r) -> p c r", r=3), ["wa"])
            for c in range(8):
                wt, wkey = load_w(wsrc(w2, 0, 8, OFF_XBC + c * 128, 128))

                def ev(g, pap, pkey):
                    a, n = GR[g]
                    if g == 0:
                        ACT(cvs[:, 3:19], pap[:, 0:16], AF.Copy, [pkey], ["cvs"])
                        ACT(cvp[:, 3:19], pap[:, 16:32], AF.Copy, [pkey], ["cvp"])
                    else:
                        ACT(cvp[:, 3 + a - 16:3 + a - 16 + n], pap, AF.Copy, [pkey], ["cvp"])
                proj_fm(wt, wkey, 128, ev)
                pt, pkey = proj_tm(wt, wkey, NT - 3, 3, 128)
                i = nxt("kvo", 2)
                V('dve', 'tensor_copy', [pkey], ["kvo%d" % i], out=kvo[i][0:3, :], in_=pt)
                DMA(dr["p_ssm_conv"][l, pi, :, c * 128:(c + 1) * 128], kvo[i][0:3, :], ["kvo%d" % i], [], stream="kvo%d" % i, eng='pool')
                if si is not None:
                    pt, pkey = proj_tm(wt, wkey, 13, 3, 128)
                    i = nxt("kvo", 2)
                    V('dve', 'tensor_copy', [pkey], ["kvo%d" % i], out=kvo[i][0:3, :], in_=pt)
                    DMA(dr["s_ssm_conv"][l, si, :, c * 128:(c + 1) * 128], kvo[i][0:3, :], ["kvo%d" % i], [], stream="kvo%d" % i, eng='pool')
                    V('dve', 'tensor_copy', ["wa"], ["cvs"], out=cvs[:, 0:3], in_=wk["wa"][:, 3 * c:3 * c + 3])
                srcs = [(cvp, LP, 16, "cvp")] + ([(cvs, SSEQ, 0, "cvs")] if si is not None else [])
                for (cv_, ln, col0, ckey) in srcs:
                    pos = 0
                    while pos < ln:
                        m = min(512, ln - pos)
                        acc = wk["wb"][:, 0:m]
                        V('dve', 'tensor_scalar', [ckey, "cw"], ["wb"], out=acc, in0=cv_[:, 3 + pos:3 + pos + m], scalar1=cw[:, c, 3:4], scalar2=cb[:, c:c + 1], op0=ALU.mult, op1=ALU.add)
                        for i in range(3):
                            V('dve', 'scalar_tensor_tensor', [ckey, "cw", "wb"], ["wb"], out=acc, in0=cv_[:, i + pos:i + pos + m], scalar=cw[:, c, i:i + 1], in1=acc, op0=ALU.mult, op1=ALU.add)
                        g0 = grp_of_col(col0 + pos); g1 = grp_of_col(col0 + pos + m - 1)
                        sl = c
                        ACT(slot(sl)[:, col0 + pos:col0 + pos + m], acc, AF.Silu, ["wb"], ["SL%d_%d" % (sl // 2, g) for g in range(g0, g1 + 1)])
                        pos += m
            V('pool', 'memset', [], ["cvp"], ap=cvp[0:64, :], constant=0.0)
            i_ = nxt("wst", 1); j_ = nxt("wbf", 2)
            DMA(wst[i_][:, 0:8, 0:8], wsrc(w2, 0, 8, OFF_DT, 8), [], ["wst%d" % i_], stream="wst%d" % i_)
            V('pool', 'tensor_copy', ["wst%d" % i_], ["wbf%d" % j_], out=wbf[j_][:, 0:8, 0:8], in_=wst[i_][:, 0:8, 0:8])

            def evdt(g, pap, pkey):
                a, n = GR[g]
                ACT(wk["wa"][0:8, 0:n], pap, AF.Exp, [pkey, "dtb"], ["wa"], bias=dtb[:, 0:1])
                ACT(dcT[0:8, a:a + n], wk["wa"][0:8, 0:n], AF.Ln, ["wa"], ["cvp"], bias=1.0)
            proj_fm(wbf[j_][:, 0:8, 0:8], "wbf%d" % j_, 8, evdt)
            subs = [('p', p_blocks(), list(range(17)))] + ([('s', s_blocks(), [16])] if si is not None else [])
            for kind, blocks, bl in subs:
                if kind == 'p':
                    V('pool', 'memset', [], ["hst"], ap=hst[:], constant=0.0)
                    V('pool', 'memset', [], ["hbfp"], ap=hbfp[:], constant=0.0)
                else:
                    for q in range(4):
                        i = nxt("stg", 2)
                        DMA(stg[i][:, 0:128], dr["state_ssm"][l, si].rearrange("h p n -> (h p) n")[q * 128:(q + 1) * 128, :], [], ["stg%d" % i], stream="stg%d" % i)
                        TR(ps[0][:, 0:128], stg[i][:, 0:128], identf[:], ["stg%d" % i, "c_ident"], [pk(0)])
                        V('dve', 'tensor_copy', [pk(0)], ["hst"], out=hst[:, 2 * q:2 * q + 2, :], in_=ps[0][:, 0:128].rearrange("p (h q) -> p h q", q=64))
                    for hd in range(8):
                        V('pool', 'tensor_copy', ["hst"], ["hbfp"], out=hbfp[:, hd, 64 * (hd % 2):64 * (hd % 2) + 64], in_=hst[:, hd, :])
                for b in bl:
                    kp, n, ca = blocks[b]
                    g = grp_of_col(ca)
                    V('dve', 'tensor_scalar', ["cvp", "dtb"], ["wa"], out=wk["wa"][0:8, 0:n], in0=dcT[0:8, ca:ca + n], scalar1=dtb[:, 3:4], scalar2=None, op0=ALU.mult)
                    V('dve', 'tensor_tensor_scan', ["wa", "ones8"], ["cvp"], out=dcT[32:40, ca:ca + n], data0=ones8[:, 0:n], data1=wk["wa"][0:8, 0:n], initial=0.0, op0=ALU.mult, op1=ALU.add)
                    TR(ps[0][0:n, 0:64], dcT[:, ca:ca + n], identf[0:64, 0:64], ["cvp", "c_ident"], [pk(0)])
                    V('dve', 'tensor_copy', [pk(0)], ["dctm"], out=dctm[0:n, :], in_=ps[0][0:n, 0:64])
                    V('dve', 'tensor_scalar', ["dctm"], ["negcs"], out=negcs[0:n, :], in0=dctm[0:n, 32:40], scalar1=-1.0, scalar2=None, op0=ALU.mult)
                    psb = ps[1][:, :].bitcast(BF16)
                    for c in range(4):
                        TR(psb[0:n, c * 128:(c + 1) * 128], slot(c)[:, ca:ca + n], identb[:], ["SL%d_%d" % (c // 2, g), "identb"], [pk(1)])
                    for hd in range(8):
                        V('dve', 'tensor_scalar', [pk(1), "dctm"], ["xdtp"], out=xdtp[0:n, hd, 64 * (hd % 2):64 * (hd % 2) + 64], in0=psb[0:n, hd * 64:(hd + 1) * 64],
                          scalar1=dctm[0:n, hd:hd + 1], scalar2=None, op0=ALU.mult)
                    psb2 = ps[0][:, :].bitcast(BF16)
                    for g2 in range(2):
                        TR(psb2[0:n, g2 * 128:(g2 + 1) * 128], slot(4 + g2)[:, ca:ca + n], identb[:], ["SL2_%d" % g, "identb"], [pk(0)])
                    V('dve', 'tensor_copy', [pk(0)], ["btm"], out=btm[0:n, :, :], in_=psb2[0:n, 0:256].rearrange("p (g m) -> p g m", m=128))
                    for g2 in range(2):
                        MM(ps[2][0:n, g2 * 128:g2 * 128 + n], slot(4 + g2)[:, ca:ca + n], slot(6 + g2)[:, ca:ca + n], True, True, ["SL2_%d" % g, "SL3_%d" % g], [pk(2)])
                    for hd in range(8):
                        g2 = hd // 4
                        MM(ps[3][:, 0:n], selT[:, hd, :], dcT[:, ca:ca + n], True, True, ["selT", "cvp"], [pk(3)])
                        MM(ps[3][0:n, 128:128 + n], selT[:, hd, 0:n], dcT[:, ca:ca + n], True, False, ["selT", "cvp"], [pk(3)])
                        MM(ps[3][0:n, 128:128 + n], identf[0:n, 0:n], ssdmaskneg[0:n, 0:n], False, True, ["c_ident", "c_ssdmaskneg"], [pk(3)])
                        ACT(erow[:, 0:n], ps[3][:, 0:n], AF.Exp, [pk(3)], ["wc"])
                        ACT(ltb[0:n, 0:n], ps[3][0:n, 128:128 + n], AF.Exp, [pk(3), "negcs"], ["wc"], bias=negcs[0:n, hd:hd + 1])
                        V('dve', 'tensor_tensor', ["wc", pk(2)], ["mtb"], out=mtb[0:n, 0:n], in0=ps[2][0:n, g2 * 128:g2 * 128 + n], in1=ltb[0:n, 0:n], op=ALU.mult)
                        V('pool', 'tensor_tensor', ["wc", "SL3_%d" % g], ["cst"], out=cst[:, 0:n], in0=slot(6 + g2)[:, ca:ca + n], in1=erow[:, 0:n], op=ALU.mult)
                        V('dve', 'tensor_scalar', ["xdtp", "wc"], ["xdd"], out=xdd[0:n, :], in0=xdtp[0:n, hd, 64 * (hd % 2):64 * (hd % 2) + 64], scalar1=ltb[0:n, n - 1:n], scalar2=None, op0=ALU.mult)
                        c2 = hd // 2
                        MM(ps[4][:, c2 * 128:c2 * 128 + n], xdtp[0:n, hd, :], mtb[0:n, 0:n], hd % 2 == 0, False, ["xdtp", "mtb"], [pk(4)])
                        MM(ps[4][:, c2 * 128:c2 * 128 + n], hbfp[:, hd, :], cst[:, 0:n], False, hd % 2 == 1, ["hbfp", "cst"], [pk(4)])
                        MM(ps[5][:, hd * 64:(hd + 1) * 64], btm[0:n, g2, :], xdd[0:n, :], True, True, ["btm", "xdd"], [pk(5)])
                        V('dve', 'scalar_tensor_tensor', ["hst", "wc", pk(5)], ["hst"], out=hst[:, hd, :], in0=hst[:, hd, :], scalar=erow[:, n - 1:n], in1=ps[5][:, hd * 64:(hd + 1) * 64], op0=ALU.mult, op1=ALU.add)
                        V('pool', 'tensor_copy', ["hst"], ["hbfp"], out=hbfp[:, hd, 64 * (hd % 2):64 * (hd % 2) + 64], in_=hst[:, hd, :])
                    for c2 in range(4):
                        V('dve', 'scalar_tensor_tensor', ["SL%d_%d" % (c2 // 2, g), "dcol", pk(4)], ["wb"], out=yv[:, c2, 0:n], in0=slot(c2)[:, ca:ca + n], scalar=dcol[:, c2:c2 + 1],
                          in1=ps[4][:, c2 * 128:c2 * 128 + n], op0=ALU.mult, op1=ALU.add)
                        ACT(szb[:, c2, 0:n], actT[:, c2, ca:ca + n], AF.Silu, ["act%d_%d" % (c2, g)], ["wa"])
                        V('pool', 'tensor_tensor', ["wb", "wa"], ["wb"], out=yv[:, c2, 0:n], in0=yv[:, c2, 0:n], in1=szb[:, c2, 0:n], op=ALU.mult)
                        ACT(sqb[:, c2, 0:n], yv[:, c2, 0:n], AF.Square, ["wb"], ["ba"])
                    for g2 in range(2):
                        for cc in range(2):
                            MM(ps[6][:, g2 * 128:g2 * 128 + n], onesb[:], sqb[:, 2 * g2 + cc, 0:n], cc == 0, cc == 1, ["ba", "onesb"], [pk(6)])
                    ACT(wk["wc"][:, 0:256], ps[6][:, 0:256], AF.Sqrt, [pk(6)], ["wc"], bias=EPS_AP[:, 0:1], scale=1.0 / 256.0)
                    V('dve', 'reciprocal', ["wc"], ["wd"], out=wk["wd"][:, 0:256], in_=wk["wc"][:, 0:256])
                    for c2 in range(4):
                        g2 = c2 // 2
                        V('dve', 'scalar_tensor_tensor', ["wb", "cw", "wd"], ["act%d_%d" % (c2, g)], out=actT[:, c2, ca:ca + n], in0=yv[:, c2, 0:n], scalar=nw[:, c2:c2 + 1],
                          in1=wk["wd"][:, g2 * 128:g2 * 128 + n], op0=ALU.mult, op1=ALU.mult)
                i = nxt("stg", 2)
                for q in range(4):
                    TR(ps[0][:, q * 128:(q + 1) * 128], hst[:, 2 * q:2 * q + 2, :].rearrange("p h q -> p (h q)"), identf[:], ["hst", "c_ident"], [pk(0)])
                V('dve', 'tensor_copy', [pk(0)], ["stg%d" % i], out=stg[i][:, 0:512], in_=ps[0][:, :])
                dst = dr["p_ssm"][l, pi] if kind == 'p' else dr["s_ssm"][l, si]
                DMA(dst.rearrange("(q p) n -> p q n", p=128), stg[i][:, 0:512].rearrange("p (q n) -> p q n", n=128), ["stg%d" % i], [], stream="stg%d" % i, eng='pool')

        def ffn(l, pi, si):
            if si is not None:
                load_T(dr["state_ffn_conv"][l, si, :, :], 2, 22, wk["wa"][:, 0:44].rearrange("p (c r) -> p c r", r=2), ["wa"])
            V('pool', 'memset', [], ["cvp"], ap=cvp[:, 0:3], constant=0.0)
            for jg in range(6):
                j0 = 4 * jg
                nj = min(4, 22 - j0)
                for jj in range(nj):
                    j = j0 + jj
                    wt, wkey = load_w(wsrc(dr["ffn_w_gate"][l], 0, 8, j * 128, 128))

                    def evg(g, pap, pkey):
                        a, n = GR[g]
                        if g == 0:
                            ACT(cvs[:, 3:19], pap[:, 0:16], AF.Copy, [pkey], ["cvs"])
                            ACT(cvp[:, 3:19], pap[:, 16:32], AF.Copy, [pkey], ["cvp"])
                        else:
                            ACT(cvp[:, 3 + a - 16:3 + a - 16 + n], pap, AF.Copy, [pkey], ["cvp"])
                    proj_fm(wt, wkey, 128, evg)
                    pt, pkey = proj_tm(wt, wkey, NT - 2, 2, 128)
                    i = nxt("kvo", 2)
                    V('dve', 'tensor_copy', [pkey], ["kvo%d" % i], out=kvo[i][0:2, :], in_=pt)
                    DMA(dr["p_ffn_conv"][l, pi, :, j * 128:(j + 1) * 128], kvo[i][0:2, :], ["kvo%d" % i], [], stream="kvo%d" % i, eng='pool')
                    if si is not None:
                        pt, pkey = proj_tm(wt, wkey, 14, 2, 128)
                        i = nxt("kvo", 2)
                        V('dve', 'tensor_copy', [pkey], ["kvo%d" % i], out=kvo[i][0:2, :], in_=pt)
                        DMA(dr["s_ffn_conv"][l, si, :, j * 128:(j + 1) * 128], kvo[i][0:2, :], ["kvo%d" % i], [], stream="kvo%d" % i, eng='pool')
                        V('dve', 'tensor_copy', ["wa"], ["cvs"], out=cvs[:, 1:3], in_=wk["wa"][:, 2 * j:2 * j + 2])
                    wt, wkey = load_w(wsrc(dr["ffn_w_up"][l], 0, 8, j * 128, 128))

                    def evu(g, pap, pkey):
                        a, n = GR[g]
                        ACT(SL[:, 3, a:a + n], pap, AF.Copy, [pkey], ["SL3_%d" % g])
                    proj_fm(wt, wkey, 128, evu)
                    srcs = [(cvp, LP, 16, "cvp")] + ([(cvs, SSEQ, 0, "cvs")] if si is not None else [])
                    for (cv_, ln, col0, ckey) in srcs:
                        pos = 0
                        while pos < ln:
                            m = min(512, ln - pos)
                            acc = wk["wb"][:, 0:m]
                            V('dve', 'tensor_scalar', [ckey, "fcw"], ["wb"], out=acc, in0=cv_[:, 3 + pos:3 + pos + m], scalar1=fcw[:, j, 2:3], scalar2=fcb[:, j:j + 1], op0=ALU.mult, op1=ALU.add)
                            for i in range(2):
                                V('dve', 'scalar_tensor_tensor', [ckey, "fcw", "wb"], ["wb"], out=acc, in0=cv_[:, 1 + i + pos:1 + i + pos + m], scalar=fcw[:, j, i:i + 1], in1=acc, op0=ALU.mult, op1=ALU.add)
                            ACT(wk["wc"][:, 0:m], acc, AF.Silu, ["wb"], ["wc"])
                            c0_ = col0 + pos
                            g0 = grp_of_col(c0_); g1 = grp_of_col(c0_ + m - 1)
                            V('pool', 'tensor_tensor', ["wc"] + ["SL3_%d" % g for g in range(g0, g1 + 1)], ["act%d_%d" % (jj, g) for g in range(g0, g1 + 1)],
                              out=actT[:, jj, c0_:c0_ + m], in0=wk["wc"][:, 0:m], in1=SL[:, 3, c0_:c0_ + m], op=ALU.mult)
                            pos += m
                for c in range(8):
                    wt, wkey = load_w(wsrc(dr["ffn_w_down"][l], j0 * 128, nj, c * 128, 128))
                    for g, (a, n) in enumerate(GR):
                        if si is None and g == 0:
                            a, n = 16, 16
                        b = nxt("pj", 2)
                        for m in range(nj):
                            MM(ps[b][:, 0:n], wt[:, m, :], actT[:, m, a:a + n], m == 0, m == nj - 1, [wkey, "act%d_%d" % (m, g)], [pk(b)])
                        V('dve', 'tensor_tensor', [pk(b), "xT%d_%d" % (c, g)], ["xT%d_%d" % (c, g)], out=xT[:, c, a:a + n], in0=ps[b][:, 0:n], in1=xT[:, c, a:a + n], op=ALU.add)

        def load_x(pi, si):
            items = [(dr["meta_tokens"][:, :], 16, 16)]
            if si is not None:
                items.append((dr["x_sample"][si, :, :], 16, 0))
            for i in range(16):
                items.append((dr["x_prompt"][pi, i * 128:(i + 1) * 128, :], 128, 32 + 128 * i))
            for src, n, col in items:
                g = grp_of_col(col)
                for half in range(2):
                    i = nxt("stg", 2)
                    DMA(stg[i][0:n, :], src[:, half * 512:(half + 1) * 512], [], ["stg%d" % i], stream="stg%d" % i)
                    b = nxt("pj", 2)
                    for c in range(4):
                        TR(ps[b][:, c * 128:c * 128 + n], stg[i][0:n, c * 128:(c + 1) * 128], identf[0:n, 0:n], ["stg%d" % i, "c_ident"], [pk(b)])
                    V('dve', 'tensor_copy', [pk(b)], ["xT%d_%d" % (half * 4 + c, g) for c in range(4)], out=xT[:, half * 4:half * 4 + 4, col:col + n],
                      in_=ps[b][:, :].rearrange("p (c m) -> p c m", m=128)[:, :, 0:n])
            if si is None:
                V('pool', 'memset', [], ["xT%d_0" % c for c in range(8)], ap=xT[:, :, 0:16], constant=0.0)

        def final_out(pi, si):
            for g, (a, n) in enumerate(GR):
                rinv = rms_stats(g, lambda c: xT[:, c, a:a + n], lambda c: ["xT%d_%d" % (c, g)], 8, float(D))
                pieces = [(0, 16, 's'), (16, 16, 'm')] if g == 0 else [(j * 128, 128, 'p') for j in range(4)]
                for c in range(8):
                    V('dve', 'scalar_tensor_tensor', ["xT%d_%d" % (c, g), "wd", "gfin"], ["SLy"] + ["SL%d_%d" % (r_, g_) for r_ in range(4) for g_ in range(5)], out=SL[:, c // 2, (c % 2) * 512:(c % 2) * 512 + n], in0=xT[:, c, a:a + n],
                      scalar=gfin[:, c:c + 1], in1=rinv, op0=ALU.mult, op1=ALU.mult)
                for (po, pn, kind) in pieces:
                    if kind == 'm' or (kind == 's' and si is None):
                        continue
                    if kind == 's':
                        dst = dr["y_sample"][si, :, :]
                    else:
                        t0 = a - 32 + po
                        dst = dr["y_prompt"][pi, t0:t0 + pn, :]
                    for half in range(2):
                        i = nxt("stg", 2)
                        b = nxt("pj", 2)
                        for c in range(4):
                            cc = half * 4 + c
                            TR(ps[b][0:pn, c * 128:(c + 1) * 128], SL[:, cc // 2, (cc % 2) * 512 + po:(cc % 2) * 512 + po + pn], identf[:], ["SLy", "c_ident"] + ["SL%d_%d" % (r_, g_) for r_ in range(4) for g_ in range(5)], [pk(b)])
                        V('dve', 'tensor_copy', [pk(b)], ["stg%d" % i], out=stg[i][0:pn, :], in_=ps[b][0:pn, :])
                        DMA(dst[:, half * 512:(half + 1) * 512], stg[i][0:pn, :], ["stg%d" % i], [], stream="stg%d" % i, eng='pool')

        fsc = sb("fsc", [128, 1])
        ALLK = ["SL%d_%d" % (r_, g_) for r_ in range(4) for g_ in range(5)] + ["VA", "VB0", "VB1", "SLy", "cvp", "cvs"] + ["act%d_%d" % (m_, g_) for m_ in range(4) for g_ in range(5)]

        def fence():
            V('pool', 'memset', ALLK, ALLK + ["fsc"], ap=fsc[:], constant=0.0)
        for (pi, si) in units:
            fence()
            load_x(pi, si)
            for l in range(n_layers):
                load_layer_params(l)
                rmsnorm_to_hT(gmix)
                fence()
                if "da" in phases:
                    for h in range(4):
                        attn_head_group(l, pi, si, True, h)
                    out_proj(l, 0)
                fence()
                if "ssd" in phases:
                    ssd(l, pi, si)
                    out_proj(l, 1)
                fence()
                if "sb" in phases:
                    V('pool', 'memset', [], ["VB0"], ap=VB0, constant=0.0)
                    V('pool', 'memset', [], ["VB1"], ap=VB1, constant=0.0)
                    for h in range(4):
                        attn_head_group(l, pi, si, False, h)
                    out_proj(l, 2)
                rmsnorm_to_hT(gffn)
                fence()
                if "ffn" in phases:
                    ffn(l, pi, si)
            fence()
            final_out(pi, si)
        P.finish()
        P.emit(nc, es)
    return nc, P


def kernel(**inputs):
    consts = host_consts()
    ins = {k: np.ascontiguousarray(np.asarray(v), dtype=np.float32) for k, v in inputs.items()}
    units = [(0, 0), (1, 1), (2, None), (3, None)]
    nc, P = build(units)
    in_maps = []
    for c in range(NCORES):
        m = {}
        for k, shp in IN_SHAPES.items():
            a = ins[k]
            if k == 'x_prompt':
                a = a[4 * c:4 * c + 4]
            elif k == 'x_sample':
                a = a[2 * c:2 * c + 2]
            elif k in ('cache_da_k', 'cache_da_v', 'cache_sb_k', 'cache_sb_v', 'state_ssm', 'state_ssm_conv', 'state_ffn_conv'):
                a = a[:, 2 * c:2 * c + 2]
            m[k] = np.ascontiguousarray(a.reshape(shp))
        m.update(consts)
        in_maps.append(m)
    res = run_bass_kernel_spmd(nc, in_maps, core_ids=list(range(NCORES)))
    full = dict(
        y_prompt=(NB, SEQ, D), y_sample=(NSB, SSEQ, D),
        p_da_k=(DEPTH, NB, LP, 4, 2, 64), p_da_v=(DEPTH, NB, LP, 4, 128), p_sb_k=(DEPTH, NB, LP, 8, 64), p_sb_v=(DEPTH, NB, LP, 8, 64),
        p_ssm=(DEPTH, NB, 8, 64, 128), p_ssm_conv=(DEPTH, NB, 3, 1024), p_ffn_conv=(DEPTH, NB, 2, DFF),
        s_da_k=(DEPTH, NSB, SSEQ, 4, 2, 64), s_da_v=(DEPTH, NSB, SSEQ, 4, 128), s_sb_k=(DEPTH, NSB, SSEQ, 8, 64), s_sb_v=(DEPTH, NSB, SSEQ, 8, 64),
        s_ssm=(DEPTH, NSB, 8, 64, 128), s_ssm_conv=(DEPTH, NSB, 3, 1024), s_ffn_conv=(DEPTH, NSB, 2, DFF))
    outs = []
    for k in OUT_ORDER:
        parts = [np.asarray(res.results[c][k]) for c in range(NCORES)]
        if k.startswith('y_'):
            a = np.concatenate(parts, axis=0)
        else:
            a = np.concatenate(parts, axis=1)
        outs.append(np.ascontiguousarray(a.reshape(full[k]).astype(np.float32)))
    return tuple(outs)
```
